# Optimizing a Trainium2 kernel written in Bass

```python
import jax, jax.numpy as jnp
from jax import lax
import numpy as np

D_MODEL = 1024
BATCH = 32
SEQ = 256
DEPTH = 1
DEC_BATCH = 8
DEC_SEQ = 4096
PAST_LEN = 512

GRID_W = 64
N_HEADS = 4
QK_DIM = 64
V_DIM = 2 * QK_DIM
ATTN_W = N_HEADS * V_DIM
CONV_CH = D_MODEL // 2
DW_WIDTH = 31
D_FF = 2816
FFN_DW_WIDTH = 3
ROPE_THETA = 10000.0
EPS = 1e-6
Q_BLOCK = 128
N_MOD = 6
IN_COLS = 3 * ATTN_W + 2 * CONV_CH + 2 * D_MODEL

kernel_name = 'hybrid_diffattn_conformer_diffusion_step'


def rmsnorm(x, w):
    xf = x.astype(jnp.float32)
    y = xf * lax.rsqrt(jnp.mean(xf * xf, axis=-1, keepdims=True) + EPS)
    return (y * w.astype(jnp.float32)).astype(x.dtype)


def layernorm(x, g, b):
    xf = x.astype(jnp.float32)
    mu = jnp.mean(xf, axis=-1, keepdims=True)
    var = jnp.mean(jnp.square(xf - mu), axis=-1, keepdims=True)
    y = (xf - mu) * lax.rsqrt(var + EPS)
    return (y * g.astype(jnp.float32) + b.astype(jnp.float32)).astype(x.dtype)


def dwconv(x, w):
    return lax.conv_general_dilated(
        x, w[:, None, :].astype(x.dtype), window_strides=(1,), padding='SAME',
        dimension_numbers=('NWC', 'WIO', 'NWC'), feature_group_count=x.shape[-1])


def axial_rope_tables(T):
    rows = T // GRID_W
    row = jnp.repeat(jnp.arange(rows, dtype=jnp.float32), GRID_W)
    col = jnp.tile(jnp.arange(GRID_W, dtype=jnp.float32), rows)
    half = QK_DIM // 2
    freqs = ROPE_THETA ** (-jnp.arange(0, half, 2, dtype=jnp.float32) / half)
    ang = jnp.stack([row[:, None] * freqs, col[:, None] * freqs], axis=1)
    return jnp.cos(ang), jnp.sin(ang)


def apply_rope(x, cos, sin):
    shp = x.shape
    xr = x.reshape(shp[:-1] + (2, 2, QK_DIM // 4))
    x1 = xr[..., 0, :]
    x2 = xr[..., 1, :]
    c = cos[None, :, None, None].astype(x.dtype)
    s = sin[None, :, None, None].astype(x.dtype)
    out = jnp.stack([x1 * c - x2 * s, x1 * s + x2 * c], axis=-2)
    return out.reshape(shp)


def diff_attention(q, k, v, lam):
    B, T, H, _ = q.shape
    nblk = T // Q_BLOCK
    scale = QK_DIM ** -0.5
    k1 = k[..., :QK_DIM]
    k2 = k[..., QK_DIM:]
    qb = jnp.moveaxis(q.reshape(B, nblk, Q_BLOCK, H, 2 * QK_DIM), 1, 0)

    def block(qblk):
        s1 = jnp.einsum('bqhd,bkhd->bhqk', qblk[..., :QK_DIM], k1).astype(jnp.float32) * scale
        s2 = jnp.einsum('bqhd,bkhd->bhqk', qblk[..., QK_DIM:], k2).astype(jnp.float32) * scale
        a = jax.nn.softmax(s1, axis=-1) - lam * jax.nn.softmax(s2, axis=-1)
        return jnp.einsum('bhqk,bkhv->bqhv', a.astype(v.dtype), v)

    o = lax.map(block, qb)
    return jnp.moveaxis(o, 0, 1).reshape(B, T, H, V_DIM)


def token_mixer(h, lp, lam, lam_init, ctx_k, ctx_v, rope):
    B, T, _ = h.shape
    proj = h @ lp['w_in']
    q = proj[..., :ATTN_W].reshape(B, T, N_HEADS, 2, QK_DIM)
    k = proj[..., ATTN_W:2 * ATTN_W].reshape(B, T, N_HEADS, 2, QK_DIM)
    v = proj[..., 2 * ATTN_W:3 * ATTN_W].reshape(B, T, N_HEADS, V_DIM)
    u = proj[..., 3 * ATTN_W:3 * ATTN_W + 2 * CONV_CH]
    g = proj[..., 3 * ATTN_W + 2 * CONV_CH:]
    if rope is not None:
        q = apply_rope(q, rope[0], rope[1])
        k = apply_rope(k, rope[0], rope[1])
    q = q.reshape(B, T, N_HEADS, 2 * QK_DIM)
    k = k.reshape(B, T, N_HEADS, 2 * QK_DIM)
    if ctx_k is None:
        keys, vals = k, v
    else:
        keys = jnp.concatenate([k, ctx_k.astype(k.dtype)], axis=1)
        vals = jnp.concatenate([v, ctx_v.astype(v.dtype)], axis=1)
    o = diff_attention(q, keys, vals, lam)
    o = rmsnorm(o, lp['w_head_norm']) * (1.0 - lam_init)
    attn_out = o.reshape(B, T, ATTN_W) @ lp['w_attn_proj']
    glu = u[..., :CONV_CH] * jax.nn.sigmoid(u[..., CONV_CH:])
    cv = jax.nn.silu(layernorm(dwconv(glu, lp['w_conv_dw']), lp['conv_ln_g'], lp['conv_ln_b']))
    conv_out = cv @ lp['w_conv_proj']
    merged = jax.nn.sigmoid(g[..., :D_MODEL]) * attn_out + jax.nn.sigmoid(g[..., D_MODEL:]) * conv_out
    return merged @ lp['w_out'], k, v


def trunk_layer(x, mod, lp, lam, lam_init, ctx_k, ctx_v, rope):
    shift1, scale1, gate1, shift2, scale2, gate2 = jnp.split(mod, N_MOD, axis=-1)
    h = rmsnorm(x, lp['w_norm1']) * (1.0 + scale1) + shift1
    mix, k, v = token_mixer(h, lp, lam, lam_init, ctx_k, ctx_v, rope)
    x = x + gate1 * mix
    h = rmsnorm(x, lp['w_norm2']) * (1.0 + scale2) + shift2
    up = dwconv(h @ lp['w_up'], lp['w_ffn_dw'])
    ff = (jax.nn.silu(up[..., :D_FF]) * up[..., D_FF:]) @ lp['w_down']
    x = x + gate2 * ff
    return x, k, v


def setup_inputs(seed: int = 0) -> dict:
    key = jax.random.key(seed)
    ks = jax.random.split(key, 32)
    f = jnp.float32
    n = lambda i, shp, s: jax.random.normal(ks[i], shp, f) * s
    return {
        'x_prompt': n(0, (BATCH, SEQ, D_MODEL), 1.0),
        'x_sample': n(1, (DEC_BATCH, DEC_SEQ, D_MODEL), 1.0),
        'cache_k': n(2, (DEC_BATCH, DEPTH, PAST_LEN, N_HEADS, 2 * QK_DIM), 1.0),
        'cache_v': n(3, (DEC_BATCH, DEPTH, PAST_LEN, N_HEADS, V_DIM), 1.0),
        'c': n(4, (DEC_BATCH, D_MODEL), 1.0),
        'c_ctx': n(5, (D_MODEL,), 1.0),
        'w_ada': n(6, (DEPTH, D_MODEL, N_MOD * D_MODEL), 0.5 * D_MODEL ** -0.5),
        'b_ada': n(7, (DEPTH, N_MOD * D_MODEL), 0.01),
        'w_norm1': 1.0 + n(8, (DEPTH, D_MODEL), 0.01),
        'w_in': n(9, (DEPTH, D_MODEL, IN_COLS), D_MODEL ** -0.5),
        'lambda_q1': n(10, (DEPTH, QK_DIM), 0.1),
        'lambda_k1': n(11, (DEPTH, QK_DIM), 0.1),
        'lambda_q2': n(12, (DEPTH, QK_DIM), 0.1),
        'lambda_k2': n(13, (DEPTH, QK_DIM), 0.1),
        'w_head_norm': 1.0 + n(14, (DEPTH, V_DIM), 0.01),
        'w_attn_proj': n(15, (DEPTH, ATTN_W, D_MODEL), ATTN_W ** -0.5),
        'w_conv_dw': n(16, (DEPTH, DW_WIDTH, CONV_CH), DW_WIDTH ** -0.5),
        'conv_ln_g': 1.0 + n(17, (DEPTH, CONV_CH), 0.01),
        'conv_ln_b': n(18, (DEPTH, CONV_CH), 0.01),
        'w_conv_proj': n(19, (DEPTH, CONV_CH, D_MODEL), CONV_CH ** -0.5),
        'w_out': n(20, (DEPTH, D_MODEL, D_MODEL), D_MODEL ** -0.5),
        'w_norm2': 1.0 + n(21, (DEPTH, D_MODEL), 0.01),
        'w_up': n(22, (DEPTH, D_MODEL, 2 * D_FF), D_MODEL ** -0.5),
        'w_ffn_dw': n(23, (DEPTH, FFN_DW_WIDTH, 2 * D_FF), FFN_DW_WIDTH ** -0.5),
        'w_down': n(24, (DEPTH, D_FF, D_MODEL), D_FF ** -0.5),
        'w_final_norm': 1.0 + n(25, (D_MODEL,), 0.01),
    }


def reference(x_prompt, x_sample, cache_k, cache_v, c, c_ctx, w_ada, b_ada, w_norm1, w_in,
              lambda_q1, lambda_k1, lambda_q2, lambda_k2, w_head_norm, w_attn_proj,
              w_conv_dw, conv_ln_g, conv_ln_b, w_conv_proj, w_out, w_norm2, w_up,
              w_ffn_dw, w_down, w_final_norm):
    rope = axial_rope_tables(x_sample.shape[1])
    xp = x_prompt
    xs = x_sample
    new_k_list = []
    new_v_list = []
    for l in range(DEPTH):
        lp = dict(w_norm1=w_norm1[l], w_in=w_in[l], w_head_norm=w_head_norm[l],
                  w_attn_proj=w_attn_proj[l], w_conv_dw=w_conv_dw[l], conv_ln_g=conv_ln_g[l],
                  conv_ln_b=conv_ln_b[l], w_conv_proj=w_conv_proj[l], w_out=w_out[l],
                  w_norm2=w_norm2[l], w_up=w_up[l], w_ffn_dw=w_ffn_dw[l], w_down=w_down[l])
        lam_init = 0.8 - 0.6 * float(np.exp(-0.3 * l))
        lam = (jnp.exp(jnp.sum(lambda_q1[l].astype(jnp.float32) * lambda_k1[l].astype(jnp.float32)))
               - jnp.exp(jnp.sum(lambda_q2[l].astype(jnp.float32) * lambda_k2[l].astype(jnp.float32)))
               + lam_init)
        mod_ctx = (jax.nn.silu(c_ctx) @ w_ada[l] + b_ada[l])[None, None, :]
        mod_lat = (jax.nn.silu(c) @ w_ada[l] + b_ada[l])[:, None, :]
        xp, kp, vp = trunk_layer(xp, mod_ctx, lp, lam, lam_init, None, None, None)
        new_k_list.append(kp)
        new_v_list.append(vp)
        xs, _, _ = trunk_layer(xs, mod_lat, lp, lam, lam_init, cache_k[:, l], cache_v[:, l], rope)
    y_prompt = rmsnorm(xp, w_final_norm)
    y_sample = rmsnorm(xs, w_final_norm)
    new_k = jnp.stack(new_k_list, axis=1)
    new_v = jnp.stack(new_v_list, axis=1)
    return (y_prompt, y_sample, new_k, new_v)
```

```python
import contextlib
import os
import numpy as np
import concourse.bass as bass
import concourse.mybir as mybir
from concourse.bass_utils import run_bass_kernel_spmd

F32 = mybir.dt.float32
BF16 = mybir.dt.bfloat16
AF = mybir.ActivationFunctionType
ALU = mybir.AluOpType

EPS = 1e-6
D = 1024
NTOK = 5120
TS = 4096
TP = 256
NPR = 4
PAST = 512
LAM_INIT = 0.8 - 0.6 * 1.0
DFF = 2816
NCH_FF = 22
DEBUG = bool(int(os.environ.get("MK_DEBUG", "0")))
UPTO = int(os.environ.get("MK_UPTO", "9"))
P1 = int(os.environ.get("MK_P1", "9"))


class Buf:
    __slots__ = ("name", "w", "r", "excl")

    def __init__(self, name="", excl=False):
        self.name = name
        self.w = {}
        self.r = {}
        self.excl = excl


def PBuf(name=""):
    return Buf(name, excl=True)


class Eng:
    def __init__(self, name, h, sem):
        self.name, self.h, self.sem = name, h, sem
        self.count = 0
        self.waited = {}

    def wait(self, tk):
        sem, val = tk
        k = sem.num
        if self.waited.get(k, 0) >= val:
            return
        self.h.wait_ge(sem, val)
        self.waited[k] = val


class MK:
    def __init__(self, nc, ndma=40):
        self.nc = nc
        self._cms = []
        self.pe = Eng("pe", nc.tensor, self._sem("s_pe"))
        self.act = Eng("act", nc.scalar, self._sem("s_act"))
        self.dve = Eng("dve", nc.vector, self._sem("s_dve"))
        self.pool = Eng("pool", nc.gpsimd, self._sem("s_pool"))
        self.sp = Eng("sp", nc.sync, self._sem("s_sp"))
        self.engs = [self.pe, self.act, self.dve, self.pool, self.sp]
        self.dsems = [self._sem(f"s_dma{i}") for i in range(ndma)]
        self.dcount = [0] * ndma
        self.nsw = 12
        self.dnext_hw = 0
        self.dnext_sw = 0

    def _sem(self, name):
        cm = self.nc.semaphore(name)
        s = cm.__enter__()
        self._cms.append(cm)
        return s

    def close(self):
        for cm in reversed(self._cms):
            cm.__exit__(None, None, None)

    def _deps(self, E, reads, writes):
        me = E.sem.num
        for b in reads:
            for k, tk in b.w.items():
                E.wait(tk)
            if b.excl:
                for k, tk in b.r.items():
                    if k != me:
                        E.wait(tk)
        skip_self = (E.name == "pe")
        for b in writes:
            for k, tk in b.w.items():
                if not (skip_self and k == me):
                    E.wait(tk)
            for k, tk in b.r.items():
                if not (skip_self and k == me):
                    E.wait(tk)

    def _mark(self, tk, reads, writes):
        k = tk[0].num
        for b in reads:
            b.r[k] = tk
        for b in writes:
            b.w[k] = tk
            b.r = {}

    def op(self, E, fn, reads=(), writes=()):
        self._deps(E, reads, writes)
        ins = fn()
        ins.then_inc(E.sem, 1)
        E.count += 1
        tk = (E.sem, E.count)
        self._mark(tk, reads, writes)
        return tk

    def group(self, E, fns, reads=(), writes=()):
        self._deps(E, reads, writes)
        ins = None
        for fn in fns:
            ins = fn()
        ins.then_inc(E.sem, 1)
        E.count += 1
        tk = (E.sem, E.count)
        self._mark(tk, reads, writes)
        return tk

    def dma(self, Q, out, in_, reads=(), writes=(), **kw):
        if Q.name == "pool":
            i = self.dnext_sw
            self.dnext_sw = (self.dnext_sw + 1) % self.nsw
        else:
            i = self.nsw + self.dnext_hw
            self.dnext_hw = (self.dnext_hw + 1) % (len(self.dsems) - self.nsw)
        sem = self.dsems[i]
        if self.dcount[i] > 0:
            Q.wait((sem, self.dcount[i]))
        for b in reads:
            for k, tk in b.w.items():
                Q.wait(tk)
        for b in writes:
            for k, tk in b.w.items():
                Q.wait(tk)
            for k, tk in b.r.items():
                Q.wait(tk)
        ins = Q.h.dma_start(out=out, in_=in_, **kw)
        ins.then_inc(sem, 16)
        self.dcount[i] += 16
        tk = (sem, self.dcount[i])
        self._mark(tk, reads, writes)
        return tk

    def barrier(self):
        tks = [(e.sem, e.count) for e in self.engs if e.count > 0]
        tks += [(s, c) for s, c in zip(self.dsems, self.dcount) if c > 0]
        for e in self.engs:
            for tk in tks:
                if tk[0].num != e.sem.num:
                    e.wait(tk)


C_BADA, C_WN1, C_WN2, C_LNG, C_LNB, C_CS, C_CC = 0, 48, 56, 64, 68, 72, 80
NROWS_A = 88


def build_nc():
    nc = bass.Bass("TRN2", target_bir_lowering=False)

    def din(name, shape):
        return nc.dram_tensor(name, list(shape), F32, kind="ExternalInput").ap()

    xs = din("xs", [NTOK, D])
    ck = din("ck", [PAST, 512])
    cv = din("cv", [PAST, 512])
    cvec = din("cvec", [2 * D])
    rope_cs = din("rope_cs", [TS, 64])
    w_ada = din("w_ada", [D, 6 * D])
    b_ada = din("b_ada", [6 * D])
    w_norm1 = din("w_norm1", [D])
    w_in = din("w_in", [D, 4608])
    lq1 = din("lambda_q1", [64]); lk1 = din("lambda_k1", [64])
    lq2 = din("lambda_q2", [64]); lk2 = din("lambda_k2", [64])
    w_head_norm = din("w_head_norm", [128])
    w_attn_proj = din("w_attn_proj", [512, D])
    w_conv_dw = din("w_conv_dw", [31, 512])
    conv_ln_g = din("conv_ln_g", [512]); conv_ln_b = din("conv_ln_b", [512])
    w_conv_proj = din("w_conv_proj", [512, D])
    w_out = din("w_out", [D, D])
    w_norm2 = din("w_norm2", [D])
    w_up = din("w_up", [D, 2 * DFF])
    w_ffn_dw = din("w_ffn_dw", [3, 2 * DFF])
    w_down = din("w_down", [DFF, D])
    w_final_norm = din("w_final_norm", [D])

    y_out = nc.dram_tensor("y", [NTOK, D], F32, kind="ExternalOutput").ap()
    nk_out = nc.dram_tensor("nk", [NPR * TP, 512], F32, kind="ExternalOutput").ap()
    nv_out = nc.dram_tensor("nv", [NPR * TP, 512], F32, kind="ExternalOutput").ap()

    skind = "ExternalOutput" if DEBUG else "Internal"
    qT_scr = nc.dram_tensor("qT_scr", [4, 128, NTOK], BF16, kind=skind).ap()
    kT_scr = nc.dram_tensor("kT_scr", [4, 128, NTOK], BF16, kind=skind).ap()
    v_scr = nc.dram_tensor("v_scr", [NTOK, 512], BF16, kind=skind).ap()
    gluT_scr = nc.dram_tensor("gluT_scr", [4, 128, NTOK], BF16, kind=skind).ap()
    sgT_scr = nc.dram_tensor("sgT_scr", [16, 128, NTOK], BF16, kind=skind).ap()
    onT_scr = nc.dram_tensor("onT_scr", [4, 128, NTOK], BF16, kind=skind).ap()
    x1_scr = nc.dram_tensor("x1_scr", [NTOK, D], F32, kind=skind).ap()

    m = MK(nc)
    pe, act, dve, pool, sp = m.pe, m.act, m.dve, m.pool, m.sp
    T, V, S, G = nc.tensor, nc.vector, nc.scalar, nc.gpsimd

    NG = NTOK // 512
    B_q = [Buf(f"q{g}") for g in range(NG)]
    B_k = [Buf(f"k{g}") for g in range(NG)]
    B_v = [Buf(f"v{t}") for t in range(NTOK // 128)]
    B_glu = [Buf(f"glu{g}") for g in range(NG)]
    B_sg = [Buf(f"sg{g}") for g in range(NG)]
    B_on = [[Buf(f"on{h}_{g}") for g in range(NG)] for h in range(4)]
    B_x1 = [Buf(f"x1_{t}") for t in range(NTOK // 128)]
    out_tks = []

    with contextlib.ExitStack() as es0:
        def sb0(name, shape, dt):
            return es0.enter_context(nc.sbuf_tensor(name, list(shape), dt))

        identf = sb0("identf", [128, 128], F32)
        identb = sb0("identb", [128, 128], BF16)
        onesf = sb0("onesf", [128, 128], F32)
        colsA = sb0("colsA", [128, NROWS_A], F32)
        colsB = sb0("colsB", [128, 124], F32)
        colsC = sb0("colsC", [128, 132], F32)
        modT = sb0("modT", [128, 48, 2], F32)
        A1 = sb0("A1", [128, 2, 8], F32); A2 = sb0("A2", [128, 2, 8], F32)
        WHb = sb0("WHb", [128, 128], F32)
        lamc = sb0("lamc", [128, 4], F32)
        nhalf = sb0("nhalf", [128, 8], F32)
        epsc = sb0("epsc", [128, 8], F32)
        Bc = {k: Buf(k) for k in ["identf", "identb", "onesf", "colsA", "colsB", "colsC", "modT", "A1", "A2",
                                  "G1b", "G2b", "WFb", "WHb", "lamc", "nhalf"]}

        def rstd_op(out, in_, n, reads, writes, scale, width=1):
            m.op(act, lambda: S.activation(out=out, in_=in_, func=AF.Ln, scale=scale, bias=epsc[0:n, 0:1]), reads=list(reads) + [Bc["nhalf"]], writes=writes)
            m.op(act, lambda: S.activation(out=out, in_=out, func=AF.Exp, scale=-0.5), reads=list(writes), writes=writes)

        with contextlib.ExitStack() as es:
            def sb(name, shape, dt):
                return es.enter_context(nc.sbuf_tensor(name, list(shape), dt))

            def ps(name, shape, dt):
                return es.enter_context(nc.psum_tensor(name, list(shape), dt))

            m.op(pool, lambda: G.memset(identf[:], 0.0), writes=[Bc["identf"]])
            m.op(pool, lambda: G.affine_select(out=identf[:], in_=identf[:], compare_op=ALU.not_equal, fill=1.0,
                                               base=0, pattern=[[-1, 128]], channel_multiplier=1),
                 reads=[Bc["identf"]], writes=[Bc["identf"]])
            m.op(dve, lambda: V.tensor_copy(out=identb[:], in_=identf[:]), reads=[Bc["identf"]], writes=[Bc["identb"]])
            m.op(pool, lambda: G.memset(onesf[:], 1.0), writes=[Bc["onesf"]])
            m.op(pool, lambda: G.memset(nhalf[:], -0.5), writes=[Bc["nhalf"]])
            m.op(pool, lambda: G.memset(epsc[:], EPS), writes=[Bc["nhalf"]])

            rowsA = sb("rowsA", [128, 128], F32)
            rowsB = sb("rowsB", [128, 128], F32)
            rowsC = sb("rowsC", [128, 128], F32)
            rowsC2 = sb("rowsC2", [8, 128], F32)
            bA, bB, bC, bC2 = Buf(), Buf(), Buf(), Buf()
            m.op(pool, lambda: G.memset(rowsA[:], 0.0), writes=[bA])

            def ldrows(dst, b, r0, vec, n):
                m.dma(sp, dst[r0:r0 + n, :], vec.rearrange("(r p) -> r p", p=128), writes=[b])
            ldrows(rowsA, bA, C_BADA, b_ada, 48)
            ldrows(rowsA, bA, C_WN1, w_norm1, 8)
            ldrows(rowsA, bA, C_WN2, w_norm2, 8)
            ldrows(rowsA, bA, C_LNG, conv_ln_g, 4)
            ldrows(rowsA, bA, C_LNB, conv_ln_b, 4)
            ldrows(rowsA, bA, C_CS, cvec, 16)
            m.dma(sp, rowsB[0:124, :], w_conv_dw.rearrange("j (c p) -> (j c) p", p=128), writes=[bB])
            wf = w_ffn_dw.rearrange("j (c p) -> (j c) p", p=128)
            m.dma(sp, rowsC[:, :], wf[0:128, :], writes=[bC])
            m.dma(sp, rowsC2[0:4, :], wf[128:132, :], writes=[bC2])
            m.dma(sp, WHb[:], w_head_norm.partition_broadcast(128), writes=[Bc["WHb"]])
            lam_in = sb("lam_in", [128, 4, 64], F32)
            bL = Buf()
            for i, lv in enumerate([lq1, lk1, lq2, lk2]):
                m.dma(sp, lam_in[:, i, :], lv.partition_broadcast(128), writes=[bL])

            pc0 = ps("pc0", [128, 512], F32)
            bp0 = PBuf()
            m.group(pe, [lambda: T.transpose(out=pc0[:, 0:NROWS_A], in_=rowsA[0:NROWS_A, :], identity=identf[0:NROWS_A, 0:NROWS_A])],
                    reads=[bA, Bc["identf"]], writes=[bp0])
            m.op(dve, lambda: V.tensor_copy(out=colsA[:], in_=pc0[:, 0:NROWS_A]), reads=[bp0], writes=[Bc["colsA"]])
            m.group(pe, [lambda: T.transpose(out=pc0[:, 0:124], in_=rowsB[0:124, :], identity=identf[0:124, 0:124])],
                    reads=[bB, Bc["identf"]], writes=[bp0])
            m.op(dve, lambda: V.tensor_copy(out=colsB[:], in_=pc0[:, 0:124]), reads=[bp0], writes=[Bc["colsB"]])
            m.group(pe, [lambda: T.transpose(out=pc0[:, 0:128], in_=rowsC[:, :], identity=identf[:, :]),
                         lambda: T.transpose(out=pc0[:, 128:132], in_=rowsC2[0:4, :], identity=identf[0:4, 0:4])],
                    reads=[bC, bC2, Bc["identf"]], writes=[bp0])
            m.op(dve, lambda: V.tensor_copy(out=colsC[:], in_=pc0[:, 0:132]), reads=[bp0], writes=[Bc["colsC"]])

            lt = sb("lt", [128, 2, 64], F32)
            ls = sb("ls", [128, 4], F32)
            bls = Buf()
            m.op(dve, lambda: V.tensor_tensor(out=lt[:, 0, :], in0=lam_in[:, 0, :], in1=lam_in[:, 1, :], op=ALU.mult), reads=[bL], writes=[bls])
            m.op(dve, lambda: V.tensor_tensor(out=lt[:, 1, :], in0=lam_in[:, 2, :], in1=lam_in[:, 3, :], op=ALU.mult), reads=[bL], writes=[bls])
            m.op(dve, lambda: V.reduce_sum(out=ls[:, 0:2], in_=lt[:], axis=mybir.AxisListType.X), reads=[bls], writes=[bls])
            m.op(act, lambda: S.activation(out=ls[:, 2:4], in_=ls[:, 0:2], func=AF.Exp), reads=[bls], writes=[bls])
            m.op(dve, lambda: V.tensor_tensor(out=lamc[:, 0:1], in0=ls[:, 2:3], in1=ls[:, 3:4], op=ALU.subtract), reads=[bls], writes=[Bc["lamc"]])
            m.op(dve, lambda: V.tensor_scalar(out=lamc[:, 0:1], in0=lamc[:, 0:1], scalar1=LAM_INIT, scalar2=None, op0=ALU.add), reads=[Bc["lamc"]], writes=[Bc["lamc"]])
            m.op(dve, lambda: V.tensor_scalar(out=lamc[:, 1:2], in0=lamc[:, 0:1], scalar1=-1.0, scalar2=None, op0=ALU.mult), reads=[Bc["lamc"]], writes=[Bc["lamc"]])
            m.op(dve, lambda: V.tensor_scalar(out=WHb[:], in0=WHb[:], scalar1=(1.0 - LAM_INIT), scalar2=None, op0=ALU.mult), reads=[Bc["WHb"]], writes=[Bc["WHb"]])

            scT = sb("scT", [128, 8, 2], F32)
            bsc = Buf()
            m.op(act, lambda: S.activation(out=scT[:].rearrange("p k s -> p s k"), in_=colsA[:, C_CS:C_CS + 16].rearrange("p (s k) -> p s k", s=2), func=AF.Silu),
                 reads=[Bc["colsA"]], writes=[bsc])
            pmod_full = ps("pmod", [128, 512], F32)
            pmod = pmod_full[:, 0:96].rearrange("p (j s) -> p j s", s=2)
            bpm = PBuf()
            wa = [sb(f"wa{i}", [128, 8, 1024], F32) for i in range(2)]
            bwa = [Buf(), Buf()]
            wav = w_ada.rearrange("(k p) n -> p k n", p=128)
            for piece in range(6):
                wt_, bw = wa[piece % 2], bwa[piece % 2]
                m.dma(sp, wt_[:], wav[:, :, piece * 1024:(piece + 1) * 1024], writes=[bw])
                for jj in range(8):
                    j = piece * 8 + jj
                    m.group(pe, [(lambda k=k, jj=jj, j=j, wt_=wt_: T.matmul(pmod[:, j, :], lhsT=wt_[:, k, jj * 128:(jj + 1) * 128], rhs=scT[:, k, :],
                                                                             start=(k == 0), stop=(k == 7))) for k in range(8)],
                            reads=[bw, bsc], writes=[bpm])
            for s in range(2):
                m.op(dve, lambda s=s: V.tensor_tensor(out=modT[:, :, s], in0=pmod[:, :, s], in1=colsA[:, C_BADA:C_BADA + 48], op=ALU.add),
                     reads=[bpm, Bc["colsA"]], writes=[Bc["modT"]])
            for s in range(2):
                m.op(dve, lambda s=s: V.scalar_tensor_tensor(out=A1[:, s, :], in0=modT[:, 8:16, s], scalar=1.0, in1=colsA[:, C_WN1:C_WN1 + 8], op0=ALU.add, op1=ALU.mult),
                     reads=[Bc["modT"], Bc["colsA"]], writes=[Bc["A1"]])
                m.op(dve, lambda s=s: V.scalar_tensor_tensor(out=A2[:, s, :], in0=modT[:, 32:40, s], scalar=1.0, in1=colsA[:, C_WN2:C_WN2 + 8], op0=ALU.add, op1=ALU.mult),
                     reads=[Bc["modT"], Bc["colsA"]], writes=[Bc["A2"]])
        m.barrier()

        if UPTO >= 1:
            phase1(nc, m, locals())
        with contextlib.ExitStack() as es23:
            p3pre = {}
            if UPTO >= 2:
                phase2(nc, m, locals())
            if UPTO >= 3:
                phase3(nc, m, locals())
        if UPTO >= 4:
            phase4(nc, m, locals())

        m.barrier()
    m.close()
    return nc


def gate_bcast(nc, m, L, gt, gbuf, c0):
    pe, dve = m.pe, m.dve
    T, V = nc.tensor, nc.vector
    Bc = L.Bc
    with contextlib.ExitStack() as es:
        dg = [es.enter_context(nc.sbuf_tensor(f"dg{c0}_{i}", [128, 128], F32)) for i in range(2)]
        pg = es.enter_context(nc.psum_tensor(f"pg{c0}", [128, D], F32))
        bdg = [Buf(), Buf()]
        bpg = PBuf()
        cnt = 0
        for s in range(2):
            for c in range(8):
                dgi, bd = dg[cnt % 2], bdg[cnt % 2]
                cnt += 1
                m.op(dve, lambda: V.tensor_scalar(out=dgi[:], in0=L.identf[:], scalar1=L.modT[:, c0 + c, s:s + 1], scalar2=None, op0=ALU.mult),
                     reads=[Bc["identf"], Bc["modT"]], writes=[bd])
                m.group(pe, [lambda: T.matmul(pg[:, c * 128:(c + 1) * 128], lhsT=L.onesf[:], rhs=dgi[:], start=True, stop=True)],
                        reads=[bd, Bc["onesf"]], writes=[bpg])
            m.op(dve, lambda: V.tensor_copy(out=gt[:, s, :], in_=pg[:]), reads=[bpg], writes=[gbuf])
    m.barrier()


class _NS:
    def __init__(self, d):
        self.__dict__.update(d)


def seg_type(tok):
    return 0 if tok < TS else 1


def phase1(nc, m, ctx):
    L = _NS(ctx)
    pe, act, dve, pool, sp = m.pe, m.act, m.dve, m.pool, m.sp
    T, V, S, G = nc.tensor, nc.vector, nc.scalar, nc.gpsimd
    Bc = L.Bc
    with contextlib.ExitStack() as es:
        def sb(name, shape, dt):
            return es.enter_context(nc.sbuf_tensor(name, list(shape), dt))

        def ps(name, shape, dt):
            return es.enter_context(nc.psum_tensor(name, list(shape), dt))

        win = sb("win", [128, 8, 4608], BF16)
        bwin = [Buf(f"win{i}") for i in range(9)]
        wv = L.w_in.rearrange("(k p) n -> p k n", p=128)
        for i in range(9):
            m.dma(pool, win[:, :, i * 512:(i + 1) * 512], wv[:, :, i * 512:(i + 1) * 512], writes=[bwin[i]])

        xt = [sb(f"xt{i}", [128, D], F32) for i in range(2)]
        bxt = [Buf(), Buf()]
        junk = sb("junk", [128, D], F32); bjunk = Buf()
        st = sb("st", [128, 4], F32); bst = Buf()
        xn = sb("xn", [128, D], BF16); bxn = Buf()
        hT = sb("hT", [128, 8, 512], BF16); bhT = Buf()
        cs_t = [sb(f"cs{i}", [128, 64], F32) for i in range(3)]
        bcs = [Buf(), Buf(), Buf()]
        r1 = sb("r1", [128, 256], F32); r2 = sb("r2", [128, 256], F32); br = Buf()
        qb = sb("qb", [128, 512], BF16); bqb = Buf()
        kb = sb("kb", [128, 512], BF16); bkb = Buf()
        vb = [sb(f"vb{i}", [128, 512], BF16) for i in range(2)]
        bvb = [Buf(), Buf()]
        kf = [sb(f"kf{i}", [128, 512], F32) for i in range(2)]
        bkf = [Buf(), Buf()]
        vf = [sb(f"vf{i}", [128, 512], F32) for i in range(2)]
        bvf = [Buf(), Buf()]
        qTg = [sb(f"qTg{i}", [128, 4, 512], BF16) for i in range(2)]
        bqTg = [Buf(), Buf()]
        kTg = [sb(f"kTg{i}", [128, 4, 512], BF16) for i in range(2)]
        bkTg = [Buf(), Buf()]
        sgm = sb("sgm", [128, 512], F32); bsgm = Buf()
        gluTg = [sb(f"gluTg{i}", [128, 4, 512], BF16) for i in range(2)]
        bglu = [Buf(), Buf()]
        sgTg = [sb(f"sgTg{i}", [128, 16, 512], BF16) for i in range(2)]
        bsg = [Buf(), Buf()]

        pT = ps("pT", [128, 8, 128], BF16); bpT = PBuf()
        pq = ps("pq", [128, 512], F32); bpq = PBuf()
        pk = ps("pk", [128, 512], F32); bpk = PBuf()
        pv = ps("pv", [128, 512], F32); bpv = PBuf()
        pqT = ps("pqT", [128, 2, 4, 128], BF16); bpqT = PBuf()
        pa = ps("pa", [128, 512], F32); bpa = PBuf()
        pb = ps("pb", [128, 512], F32); bpb = PBuf()
        pgg = ps("pgg", [128, 512], F32); bpgg = PBuf()

        def load_x(tt):
            i = tt % 2
            m.dma(sp, xt[i][:], L.xs[tt * 128:(tt + 1) * 128, :], writes=[bxt[i]])
            if tt * 128 < TS:
                m.dma(sp, cs_t[tt % 3][:], L.rope_cs[tt * 128:(tt + 1) * 128, :], writes=[bcs[tt % 3]])

        def rope(src_ps, bsrc, dst_bf, bdst, cst, bcst):
            sv = src_ps[:].rearrange("p (g a b i) -> p g a b i", g=8, a=2, b=2)
            dv = dst_bf[:].rearrange("p (g a b i) -> p g a b i", g=8, a=2, b=2)
            x1, x2 = sv[:, :, :, 0, :], sv[:, :, :, 1, :]
            o1, o2 = dv[:, :, :, 0, :], dv[:, :, :, 1, :]
            cos = cst[:, 0:32].rearrange("p (a i) -> p a i", a=2).unsqueeze(1).broadcast_to([128, 8, 2, 16])
            sin = cst[:, 32:64].rearrange("p (a i) -> p a i", a=2).unsqueeze(1).broadcast_to([128, 8, 2, 16])
            t1 = r1[:].rearrange("p (g a i) -> p g a i", g=8, a=2)
            t2 = r2[:].rearrange("p (g a i) -> p g a i", g=8, a=2)
            m.op(dve, lambda: V.tensor_tensor(out=t1, in0=x1, in1=cos, op=ALU.mult), reads=[bsrc, bcst], writes=[br])
            m.op(dve, lambda: V.tensor_tensor(out=t2, in0=x2, in1=sin, op=ALU.mult), reads=[bsrc, bcst], writes=[br])
            m.op(dve, lambda: V.tensor_tensor(out=o1, in0=t1, in1=t2, op=ALU.subtract), reads=[br], writes=[bdst])
            m.op(dve, lambda: V.tensor_tensor(out=t1, in0=x1, in1=sin, op=ALU.mult), reads=[bsrc, bcst], writes=[br])
            m.op(dve, lambda: V.tensor_tensor(out=t2, in0=x2, in1=cos, op=ALU.mult), reads=[bsrc, bcst], writes=[br])
            m.op(dve, lambda: V.tensor_tensor(out=o2, in0=t1, in1=t2, op=ALU.add), reads=[br], writes=[bdst])

        hT2 = [hT, sb("hTb", [128, 8, 512], BF16)]
        bhT2 = [bhT, Buf()]
        sg_banks = [(pgg, bpgg), (pa, bpa), (pb, bpb)]

        def fm_units(g):
            gi = g % 2
            hTg, bhTg = hT2[gi], bhT2[gi]
            t0 = g * 512
            units = []
            for cc in range(4):
                def u(cc=cc):
                    ca, cb = 1536 + cc * 128, 2048 + cc * 128
                    m.group(pe, [(lambda k=k: T.matmul(pa[:], lhsT=win[:, k, ca:ca + 128], rhs=hTg[:, k, :], start=(k == 0), stop=(k == 7))) for k in range(8)],
                            reads=[bhTg, bwin[3]], writes=[bpa])
                    m.group(pe, [(lambda k=k: T.matmul(pb[:], lhsT=win[:, k, cb:cb + 128], rhs=hTg[:, k, :], start=(k == 0), stop=(k == 7))) for k in range(8)],
                            reads=[bhTg, bwin[4]], writes=[bpb])
                    m.op(act, lambda: S.activation(out=sgm[:], in_=pb[:], func=AF.Sigmoid), reads=[bpb], writes=[bsgm])
                    m.op(dve, lambda: V.tensor_tensor(out=gluTg[gi][:, cc, :], in0=pa[:], in1=sgm[:], op=ALU.mult), reads=[bpa, bsgm], writes=[bglu[gi]])
                    if cc == 3:
                        m.dma(pool, L.gluT_scr[:, :, t0:t0 + 512].rearrange("c p t -> p c t"), gluTg[gi][:], reads=[bglu[gi]], writes=[L.B_glu[g]])
                units.append(u)
            for c16 in range(16):
                def u(c16=c16):
                    cg = 2560 + c16 * 128
                    pgx, bpgx = sg_banks[c16 % 3]
                    m.group(pe, [(lambda k=k: T.matmul(pgx[:], lhsT=win[:, k, cg:cg + 128], rhs=hTg[:, k, :], start=(k == 0), stop=(k == 7))) for k in range(8)],
                            reads=[bhTg, bwin[cg // 512]], writes=[bpgx])
                    m.op(dve, lambda: V.tensor_copy(out=sgTg[gi][:, c16, :], in_=pgx[:]), reads=[bpgx], writes=[bsg[gi]])
                    if c16 == 15:
                        m.dma(pool, L.sgT_scr[:, :, t0:t0 + 512].rearrange("c p t -> p c t"), sgTg[gi][:], reads=[bsg[gi]], writes=[L.B_sg[g]])
                units.append(u)
            return units

        NT = NTOK // 128

        def front(tt):
            g, i = tt // 4, tt % 4
            s = 0 if g * 512 < TS else 1
            hTg, bhTg = hT2[g % 2], bhT2[g % 2]
            xi = tt % 2
            if tt + 1 < NT:
                load_x(tt + 1)
            m.op(act, lambda: S.activation(out=junk[:], in_=xt[xi][:], func=AF.Square, accum_out=st[:, 0:1]),
                 reads=[bxt[xi]], writes=[bjunk, bst])
            L.rstd_op(st[:, 1:2], st[:, 0:1], 128, [bst], [bst], 1.0 / D)
            m.op(dve, lambda: V.tensor_scalar(out=xn[:], in0=xt[xi][:], scalar1=st[:, 1:2], scalar2=None, op0=ALU.mult),
                 reads=[bxt[xi], bst], writes=[bxn])
            m.group(pe, [(lambda k=k: T.transpose(out=pT[:, k, :], in_=xn[:, k * 128:(k + 1) * 128], identity=L.identb[:])) for k in range(8)],
                    reads=[bxn, Bc["identb"]], writes=[bpT])
            for k in range(8):
                m.op(act, lambda k=k: S.activation(out=hTg[:, k, i * 128:(i + 1) * 128], in_=pT[:, k, :], func=AF.Identity,
                                                   scale=L.A1[:, s, k:k + 1], bias=L.modT[:, k, s:s + 1]),
                     reads=[bpT, Bc["A1"], Bc["modT"]], writes=[bhTg])

        def tm_proj(tt):
            g, i = tt // 4, tt % 4
            hTg, bhTg = hT2[g % 2], bhT2[g % 2]
            for (pp, bpp, c0, wi) in ((pq, bpq, 0, 0), (pk, bpk, 512, 1), (pv, bpv, 1024, 2)):
                m.group(pe, [(lambda k=k, pp=pp, c0=c0: T.matmul(pp[:], lhsT=hTg[:, k, i * 128:(i + 1) * 128], rhs=win[:, k, c0:c0 + 512],
                                                                 start=(k == 0), stop=(k == 7))) for k in range(8)],
                        reads=[bhTg, bwin[wi]], writes=[bpp])

        def back(tt):
            g, i = tt // 4, tt % 4
            s = 0 if g * 512 < TS else 1
            gi = g % 2
            xi = tt % 2
            if s == 0:
                rope(pq, bpq, qb, bqb, cs_t[tt % 3], bcs[tt % 3])
                rope(pk, bpk, kb, bkb, cs_t[tt % 3], bcs[tt % 3])
            else:
                m.op(dve, lambda: V.tensor_copy(out=qb[:], in_=pq[:]), reads=[bpq], writes=[bqb])
                m.op(dve, lambda: V.tensor_copy(out=kb[:], in_=pk[:]), reads=[bpk], writes=[bkb])
                pj = tt % 2
                m.op(act, lambda: S.activation(out=kf[pj][:], in_=pk[:], func=AF.Identity), reads=[bpk], writes=[bkf[pj]])
                m.op(act, lambda: S.activation(out=vf[pj][:], in_=pv[:], func=AF.Identity), reads=[bpv], writes=[bvf[pj]])
                r0 = tt * 128 - TS
                L.out_tks.append(m.dma(pool, L.nk_out[r0:r0 + 128, :], kf[pj][:], reads=[bkf[pj]]))
                L.out_tks.append(m.dma(pool, L.nv_out[r0:r0 + 128, :], vf[pj][:], reads=[bvf[pj]]))
            vi = tt % 2
            m.op(act, lambda: S.activation(out=vb[vi][:], in_=pv[:], func=AF.Identity), reads=[bpv], writes=[bvb[vi]])
            m.dma(pool, L.v_scr[tt * 128:(tt + 1) * 128, :], vb[vi][:], reads=[bvb[vi]], writes=[L.B_v[tt]])
            m.group(pe, [(lambda h=h: T.transpose(out=pqT[:, 0, h, :], in_=qb[:, h * 128:(h + 1) * 128], identity=L.identb[:])) for h in range(4)]
                    + [(lambda h=h: T.transpose(out=pqT[:, 1, h, :], in_=kb[:, h * 128:(h + 1) * 128], identity=L.identb[:])) for h in range(4)],
                    reads=[bqb, bkb, Bc["identb"]], writes=[bpqT])
            m.op(act, lambda: S.activation(out=qTg[gi][:, :, i * 128:(i + 1) * 128], in_=pqT[:, 0, :, :], func=AF.Identity), reads=[bpqT], writes=[bqTg[gi]])
            m.op(act, lambda: S.activation(out=kTg[gi][:, :, i * 128:(i + 1) * 128], in_=pqT[:, 1, :, :], func=AF.Identity), reads=[bpqT], writes=[bkTg[gi]])
            if i == 3:
                t0 = g * 512
                m.dma(pool, L.qT_scr[:, :, t0:t0 + 512].rearrange("h p t -> p h t"), qTg[gi][:], reads=[bqTg[gi]], writes=[L.B_q[g]])
                m.dma(pool, L.kT_scr[:, :, t0:t0 + 512].rearrange("h p t -> p h t"), kTg[gi][:], reads=[bkTg[gi]], writes=[L.B_k[g]])

        load_x(0)
        pending = []
        front(0)
        for tt in range(NT):
            g, i = tt // 4, tt % 4
            tm_proj(tt)
            if tt + 1 < NT and not ((tt + 1) % 4 == 0 and pending):
                front(tt + 1)
                fronted = True
            else:
                fronted = False
            for _ in range(5):
                if pending:
                    pending.pop(0)()
            back(tt)
            if i == 3:
                while pending:
                    pending.pop(0)()
                pending = fm_units(g)
            if tt + 1 < NT and not fronted:
                front(tt + 1)
        while pending:
            pending.pop(0)()
    m.barrier()


def phase2(nc, m, ctx):
    L = _NS(ctx)
    pe, act, dve, pool, sp = m.pe, m.act, m.dve, m.pool, m.sp
    T, V, S, G = nc.tensor, nc.vector, nc.scalar, nc.gpsimd
    Bc = L.Bc
    VW = 132
    with contextlib.ExitStack() as es:
        def sb(name, shape, dt):
            return es.enter_context(nc.sbuf_tensor(name, list(shape), dt))

        def ps(name, shape, dt):
            return es.enter_context(nc.psum_tensor(name, list(shape), dt))

        if UPTO >= 3:
            def sb23(name, shape, dt):
                return L.es23.enter_context(nc.sbuf_tensor(name, list(shape), dt))
            G1b = sb23("G1b", [128, 2, D], F32)
            gate_bcast(nc, m, L, G1b, Bc["G1b"], 16)
            wap = sb23("wap", [128, 4, D], BF16); bwap = Buf()
            wcp = sb23("wcp", [128, 4, D], BF16); bwcp = Buf()
            wout = sb23("wout", [128, 8, D], BF16); bwout = Buf()
            m.dma(pool, wap[:], L.w_attn_proj.rearrange("(k p) n -> p k n", p=128), writes=[bwap])
            m.dma(pool, wcp[:], L.w_conv_proj.rearrange("(k p) n -> p k n", p=128), writes=[bwcp])
            m.dma(pool, wout[:], L.w_out.rearrange("(k p) n -> p k n", p=128), writes=[bwout])
            dgc = sb23("dgc", [128, 124, 128], BF16); bdgc = Buf()
            for jc in range(124):
                m.op(dve, lambda jc=jc: V.tensor_scalar(out=dgc[:, jc, :], in0=L.identf[:], scalar1=L.colsB[:, jc:jc + 1], scalar2=None, op0=ALU.mult),
                     reads=[Bc["identf"], Bc["colsB"]], writes=[bdgc])
            L.p3pre.update(dict(G1b=G1b, wap=wap, bwap=bwap, wcp=wcp, bwcp=bwcp, wout=wout, bwout=bwout, dgc=dgc, bdgc=bdgc))
        NKT = (TS + PAST) // 128
        kTh = [sb(f"kTh{i}", [128, TS + PAST], BF16) for i in range(2)]
        bkTh = [Buf(), Buf()]
        Vh = [sb(f"Vh{i}", [128, NKT, VW], BF16) for i in range(2)]
        bVh = [Buf(), Buf()]
        ckf = sb("ckf", [128, 4, 128], F32); bckf = Buf()
        qTt = [sb(f"qTt{i}", [128, 512], BF16) for i in range(2)]
        bqTt = [Buf(), Buf()]
        PT = [sb(f"PT{i}", [128, 1024], BF16) for i in range(3)]
        bPT = [Buf(), Buf(), Buf()]
        sm = sb("sm", [128, 16], F32); bsm = Buf()
        oa = sb("oa", [128, 4, 128], F32); boa = Buf()
        ob = sb("ob", [128, 4, 128], F32); bob = Buf()
        onb = sb("onb", [128, 4, 128], BF16); bonb = Buf()
        ocp = [sb(f"ocp{i}", [128, 8, 132], F32) for i in range(2)]
        bocp = [Buf(), Buf()]
        onTg = [sb(f"onTg{i}", [128, 512], BF16) for i in range(2)]
        bonTg = [Buf(), Buf()]

        pS = [ps(f"pS{i}", [128, 2, 512], F32) for i in range(2)]
        bpS = [PBuf(), PBuf()]
        pO = ps("pO", [128, 3, 512], F32); bpO = PBuf()
        pX = ps("pX", [128, 512], F32); bpX = PBuf()
        pXb = pX[:].bitcast(BF16)

        for i in range(2):
            m.op(pool, lambda i=i: G.memset(Vh[i][:, :, 128:VW], 1.0), writes=[bVh[i]])

        def acc_ap(a):
            return pO[:, a // 3, (a % 3) * VW:(a % 3) * VW + 129]

        pt_ctr = [0]

        qpre = {}
        s0_done = {}
        tail_pending = []
        mid_pending = []

        def s_group_x(hi_, qi_, nq_, kt):
            pi = kt % 2
            m.group(pe, [
                lambda: T.matmul(pS[pi][:, 0, 0:nq_], lhsT=kTh[hi_][0:64, kt * 128:(kt + 1) * 128], rhs=qTt[qi_][0:64, 0:nq_], start=True, stop=True),
                lambda: T.matmul(pS[pi][:, 1, 0:nq_], lhsT=kTh[hi_][64:128, kt * 128:(kt + 1) * 128], rhs=qTt[qi_][64:128, 0:nq_], start=True, stop=True),
            ], reads=[bkTh[hi_], bqTt[qi_]], writes=[bpS[pi]])

        def attend(hi, tok0, nq, nkt, on_buf_i, dest_tok0, h, Bon, nxt_q=None, nxt_s=None):
            nqc = nq // 128
            g = tok0 // 512
            qi = on_buf_i
            if not qpre.get((h, tok0)):
                m.dma(sp, qTt[qi][:, 0:nq], L.qT_scr[h, :, tok0:tok0 + nq], reads=[L.B_q[g]], writes=[bqTt[qi]])
            if nxt_q is not None:
                h2_, t2_, n2_ = nxt_q
                m.dma(sp, qTt[1 - qi][:, 0:n2_], L.qT_scr[h2_, :, t2_:t2_ + n2_], reads=[L.B_q[t2_ // 512]], writes=[bqTt[1 - qi]])
                qpre[(h2_, t2_)] = True

            def s_group(kt):
                s_group_x(hi, qi, nq, kt)

            if not s0_done.get((h, tok0)):
                s_group(0)
            for kt in range(nkt):
                if kt + 1 < nkt:
                    s_group(kt + 1)
                elif nxt_s is not None and nkt % 2 == 0:
                    hi2_, nq2_, h2_, t2_ = nxt_s
                    s_group_x(hi2_, 1 - qi, nq2_, 0)
                    s0_done[(h2_, t2_)] = True
                pi = kt % 2
                pti = pt_ctr[0] % 3
                pt_ctr[0] += 1
                m.op(act, lambda: S.activation(out=PT[pti][:].rearrange("p (a q) -> p a q", a=2)[:, :, 0:nq], in_=pS[pi][:, :, 0:nq], func=AF.Exp, scale=0.125),
                     reads=[bpS[pi]], writes=[bPT[pti]])
                fns = []
                seen = set()
                for smx in range(2):
                    for qc in range(nqc):
                        a = smx * 4 + qc
                        first = (kt == 0) and ((a // 3) not in seen)
                        seen.add(a // 3)
                        fns.append(lambda a=a, smx=smx, qc=qc, first=first: T.matmul(
                            acc_ap(a), lhsT=PT[pti][:, smx * 512 + qc * 128: smx * 512 + (qc + 1) * 128], rhs=Vh[hi][:, kt, 0:129],
                            start=first, stop=(kt == nkt - 1), skip_group_check=True))
                m.group(pe, fns, reads=[bPT[pti], bVh[hi]], writes=[bpO])
                if kt == min(10, nkt - 1) and mid_pending:
                    mid_pending.pop(0)()
                if kt == min(13, nkt - 1) and tail_pending and not mid_pending:
                    tail_pending.pop(0)()
            oi = on_buf_i
            oc = ocp[oi]
            boc = bocp[oi]
            if nqc == 4:
                for bk, (a0, na) in enumerate(((0, 3), (3, 3), (6, 2))):
                    m.op(dve, lambda: V.tensor_copy(out=oc[:, a0:a0 + na, 0:129], in_=pO[:, bk, 0:na * VW].rearrange("p (a w) -> p a w", w=VW)[:, :, 0:129]),
                         reads=[bpO], writes=[boc])
            else:
                m.op(dve, lambda: V.tensor_copy(out=oc[:, 0:2, 0:129], in_=pO[:, 0, 0:2 * VW].rearrange("p (a w) -> p a w", w=VW)[:, :, 0:129]), reads=[bpO], writes=[boc])
                m.op(dve, lambda: V.tensor_copy(out=oc[:, 4:6, 0:129], in_=pO[:, 1, VW:3 * VW].rearrange("p (a w) -> p a w", w=VW)[:, :, 0:129]), reads=[bpO], writes=[boc])
            O1 = oc[:, 0:nqc, 0:128]
            O2 = oc[:, 4:4 + nqc, 0:128]
            m.op(dve, lambda: V.reciprocal(out=sm[:, 0:nqc], in_=oc[:, 0:nqc, 128]), reads=[boc], writes=[bsm])
            m.op(dve, lambda: V.reciprocal(out=sm[:, 4:4 + nqc], in_=oc[:, 4:4 + nqc, 128]), reads=[boc], writes=[bsm])
            m.op(dve, lambda: V.tensor_scalar(out=sm[:, 4:4 + nqc], in0=sm[:, 4:4 + nqc], scalar1=L.lamc[:, 1:2], scalar2=None, op0=ALU.mult),
                 reads=[bsm, Bc["lamc"]], writes=[bsm])
            oav = oa[:, 0:nqc, :]
            obv = ob[:, 0:nqc, :]
            m.op(dve, lambda: V.tensor_tensor(out=oav, in0=O1, in1=sm[:, 0:nqc].unsqueeze(2).broadcast_to([128, nqc, 128]), op=ALU.mult),
                 reads=[boc, bsm], writes=[boa])
            m.op(dve, lambda: V.tensor_tensor(out=obv, in0=O2, in1=sm[:, 4:4 + nqc].unsqueeze(2).broadcast_to([128, nqc, 128]), op=ALU.mult),
                 reads=[boc, bsm], writes=[bob])
            m.op(dve, lambda: V.tensor_tensor(out=obv, in0=obv, in1=oav, op=ALU.add), reads=[bob, boa], writes=[bob])
            m.op(dve, lambda: V.tensor_tensor(out=oav, in0=obv, in1=obv, op=ALU.mult), reads=[bob], writes=[boa])
            m.op(dve, lambda: V.reduce_sum(out=sm[:, 8:8 + nqc], in_=oav, axis=mybir.AxisListType.X), reads=[boa], writes=[bsm])
            m.op(dve, lambda: V.tensor_scalar(out=sm[:, 12:12 + nqc], in0=sm[:, 8:8 + nqc], scalar1=1.0 / 128, scalar2=EPS, op0=ALU.mult, op1=ALU.add),
                 reads=[bsm], writes=[bsm])

            def mid():
                m.op(act, lambda: S.activation(out=sm[:, 12:12 + nqc], in_=sm[:, 12:12 + nqc], func=AF.Ln), reads=[bsm], writes=[bsm])
                m.op(act, lambda: S.activation(out=sm[:, 12:12 + nqc], in_=sm[:, 12:12 + nqc], func=AF.Exp, scale=-0.5), reads=[bsm], writes=[bsm])
                m.op(dve, lambda: V.tensor_tensor(out=obv, in0=obv, in1=sm[:, 12:12 + nqc].unsqueeze(2).broadcast_to([128, nqc, 128]), op=ALU.mult),
                     reads=[bob, bsm], writes=[bob])
                m.op(dve, lambda: V.tensor_tensor(out=onb[:, 0:nqc, :], in0=obv, in1=L.WHb[:].unsqueeze(1).broadcast_to([128, nqc, 128]), op=ALU.mult),
                     reads=[bob, Bc["WHb"]], writes=[bonb])
            mid_pending.append(mid)

            def tail():
                m.group(pe, [(lambda qc=qc: T.transpose(out=pXb[:, qc * 128:(qc + 1) * 128], in_=onb[:, qc, :], identity=L.identb[:])) for qc in range(nqc)],
                        reads=[bonb, Bc["identb"]], writes=[bpX])
                m.op(dve, lambda: V.tensor_copy(out=onTg[on_buf_i][:, 0:nq], in_=pXb[:, 0:nq]), reads=[bpX], writes=[bonTg[on_buf_i]])
                m.dma(pool, L.onT_scr[h, :, dest_tok0:dest_tok0 + nq], onTg[on_buf_i][:, 0:nq], reads=[bonTg[on_buf_i]], writes=[Bon])
            tail_pending.append(tail)

        cnt = 0

        def load_sample_head(h):
            hi = h % 2
            m.dma(sp, kTh[hi][:, 0:TS], L.kT_scr[h, :, 0:TS], reads=L.B_k[0:8], writes=[bkTh[hi]])
            for vq in range(4):
                m.dma(sp, Vh[hi][:, vq * 8:(vq + 1) * 8, 0:128],
                      L.v_scr[vq * 1024:(vq + 1) * 1024, h * 128:(h + 1) * 128].rearrange("(t p) d -> p t d", p=128),
                      reads=L.B_v[vq * 8:(vq + 1) * 8], writes=[bVh[hi]])
            m.dma(pool, Vh[hi][:, 32:36, 0:128], L.cv[:, h * 128:(h + 1) * 128].rearrange("(t p) d -> p t d", p=128), writes=[bVh[hi]])
            m.dma(sp, ckf[:], L.ck[:, h * 128:(h + 1) * 128].rearrange("(t p) d -> p t d", p=128), writes=[bckf])
            m.group(pe, [(lambda t=t: T.transpose(out=pX[:, t * 128:(t + 1) * 128], in_=ckf[:, t, :], identity=L.identf[:])) for t in range(4)],
                    reads=[bckf, Bc["identf"]], writes=[bpX])
            m.op(dve, lambda: V.tensor_copy(out=kTh[hi][:, TS:TS + PAST], in_=pX[:]), reads=[bpX], writes=[bkTh[hi]])

        def load_prompt_head(j, h):
            p0 = TS + j * TP
            hi = (j * 4 + h) % 2
            g = p0 // 512
            m.dma(sp, kTh[hi][:, 0:TP], L.kT_scr[h, :, p0:p0 + TP], reads=[L.B_k[g]], writes=[bkTh[hi]])
            m.dma(sp, Vh[hi][:, 0:2, 0:128], L.v_scr[p0:p0 + TP, h * 128:(h + 1) * 128].rearrange("(t p) d -> p t d", p=128),
                  reads=L.B_v[p0 // 128:p0 // 128 + 2], writes=[bVh[hi]])

        work = []
        for h in range(4):
            for qt in range(8):
                work.append(("s", 0, h, qt * 512, 512, NKT))
        for j in range(NPR):
            for h in range(4):
                work.append(("p", j, h, TS + j * TP, TP, 2))
        load_sample_head(0)
        loaded = {("s", 0, 0)}
        for wi_, (kind, j, h, tok0, nq, nkt) in enumerate(work):
            for (k2, j2, h2, *_r) in work[wi_ + 1:]:
                if (k2, j2, h2) != (kind, j, h):
                    if (k2, j2, h2) not in loaded:
                        if k2 == "s":
                            load_sample_head(h2)
                        else:
                            load_prompt_head(j2, h2)
                        loaded.add((k2, j2, h2))
                    break
            hi = (h % 2) if kind == "s" else ((j * 4 + h) % 2)
            nx = work[wi_ + 1] if wi_ + 1 < len(work) else None
            nxt_q = (nx[2], nx[3], nx[4]) if nx is not None else None
            g = tok0 // 512
            nxt_s = None
            if nx is not None:
                hi_n = (nx[2] % 2) if nx[0] == "s" else ((nx[1] * 4 + nx[2]) % 2)
                nxt_s = (hi_n, nx[4], nx[2], nx[3])
            attend(hi, tok0, nq, nkt, cnt % 2, tok0, h, L.B_on[h][g], nxt_q=nxt_q, nxt_s=nxt_s)
            cnt += 1
        while mid_pending:
            mid_pending.pop(0)()
        while tail_pending:
            tail_pending.pop(0)()
    m.barrier()


def phase3(nc, m, ctx):
    L = _NS(ctx)
    pe, act, dve, pool, sp = m.pe, m.act, m.dve, m.pool, m.sp
    T, V, S, G = nc.tensor, nc.vector, nc.scalar, nc.gpsimd
    Bc = L.Bc
    GW = 572
    with contextlib.ExitStack() as es:
        def sb(name, shape, dt):
            return es.enter_context(nc.sbuf_tensor(name, list(shape), dt))

        def ps(name, shape, dt):
            return es.enter_context(nc.psum_tensor(name, list(shape), dt))

        P3 = L.p3pre
        G1b, wap, bwap, wcp, bwcp, wout, bwout, dgc, bdgc = (P3[k] for k in ("G1b", "wap", "bwap", "wcp", "bwcp", "wout", "bwout", "dgc", "bdgc"))

        gluw = [sb(f"gluw{i}", [128, 4, GW], BF16) for i in range(2)]
        bgluw = [Buf(), Buf()]
        cvo = sb("cvo", [128, 4, 512], F32); bcvo = Buf()
        sq = sb("sq", [128, 4, 512], F32); bsq = Buf()
        mean = sb("mean", [128, 512], F32); bmean = Buf()
        rstd = sb("rstd", [128, 512], F32); brstd = Buf()
        tmp = sb("tmp3", [128, 512], F32); btmp = Buf()
        cvT = sb("cvT", [128, 4, 512], BF16); bcvT = Buf()
        onTw = [sb(f"onTw{i}", [128, 4, 512], BF16) for i in range(2)]
        bonTw = [Buf(), Buf()]
        sgw = [sb(f"sgw{i}", [128, 16, 512], BF16) for i in range(2)]
        bsgw = [Buf(), Buf()]
        t1 = sb("t1", [128, 512], F32); bt1 = Buf()
        t2 = sb("t2", [128, 512], F32); bt2 = Buf()
        mergedT = sb("mergedT", [128, 8, 512], BF16); bmg = Buf()
        xt = [sb(f"x3t{i}", [128, D], F32) for i in range(2)]
        bxt = [Buf(), Buf()]
        x1t = [sb(f"x1t{i}", [128, D], F32) for i in range(2)]
        bx1t = [Buf(), Buf()]

        pc = ps("pc", [128, 512], F32); bpc = PBuf()
        pc2 = ps("pc2", [128, 512], F32); bpc2 = PBuf()
        pS1 = ps("pSa", [128, 512], F32); bpS1 = PBuf()
        pS2 = ps("pSb", [128, 512], F32); bpS2 = PBuf()
        pA = ps("pA", [128, 512], F32); bpA = PBuf()
        pB = ps("pB", [128, 512], F32); bpB = PBuf()
        pM = [ps(f"pM{i}", [128, 512], F32) for i in range(2)]
        bpM = [PBuf(), PBuf()]

        def units_of(g):
            t0 = g * 512
            if t0 < TS:
                return [(t0, 512, 0, TS)]
            return [(t0, 256, t0, t0 + 256), (t0 + 256, 256, t0 + 256, t0 + 512)]

        def load_group(g):
            gi = g % 2
            t0 = g * 512
            units = units_of(g)
            full = all(u0 - 15 >= lo and u0 + n + 15 <= hi for (u0, n, lo, hi) in units)
            if not full:
                m.op(pool, lambda: G.memset(gluw[gi][:], 0.0), writes=[bgluw[gi]])
            for ui, (u0, n, lo, hi) in enumerate(units):
                a0, a1 = max(u0 - 15, lo), min(u0 + n + 15, hi)
                c0 = ui * 286 + (a0 - (u0 - 15))
                gs = list(range(a0 // 512, (a1 - 1) // 512 + 1))
                m.dma(sp, gluw[gi][:, :, c0:c0 + (a1 - a0)], L.gluT_scr[:, :, a0:a1].rearrange("c p t -> p c t"),
                      reads=[L.B_glu[x] for x in gs], writes=[bgluw[gi]])

        def load_rest(g):
            gi = g % 2
            t0 = g * 512
            m.dma(sp, onTw[gi][:], L.onT_scr[:, :, t0:t0 + 512].rearrange("c p t -> p c t"), reads=[L.B_on[h][g] for h in range(4)], writes=[bonTw[gi]])
            m.dma(sp, sgw[gi][:], L.sgT_scr[:, :, t0:t0 + 512].rearrange("c p t -> p c t"), reads=[L.B_sg[g]], writes=[bsgw[gi]])
            for q4 in range(4):
                m.op(act, lambda q4=q4: S.activation(out=sgw[gi][:, q4 * 4:(q4 + 1) * 4, :], in_=sgw[gi][:, q4 * 4:(q4 + 1) * 4, :], func=AF.Sigmoid),
                     reads=[bsgw[gi]], writes=[bsgw[gi]])

        def load_x(tt):
            m.dma(sp, xt[tt % 2][:], L.xs[tt * 128:(tt + 1) * 128, :], writes=[bxt[tt % 2]])

        cvT2 = [cvT, sb("cvTb", [128, 4, 512], BF16)]
        bcvT2 = [bcvT, Buf()]

        def conv_ln(g):
            gi = g % 2
            s = 0 if g * 512 < TS else 1
            units = units_of(g)
            for cc in range(4):
                pcx, bpcx = (pc, bpc) if cc % 2 == 0 else (pc2, bpc2)
                for ui, (u0, n, lo, hi) in enumerate(units):
                    m.group(pe, [(lambda j=j: T.matmul(pcx[:, ui * 256:ui * 256 + n], lhsT=dgc[:, j * 4 + cc, :], rhs=gluw[gi][:, cc, ui * 286 + j:ui * 286 + j + n],
                                                       start=(j == 0), stop=(j == 30))) for j in range(31)],
                            reads=[bdgc, bgluw[gi]], writes=[bpcx])
                m.op(act, lambda: S.activation(out=cvo[:, cc, :], in_=pcx[:], func=AF.Identity), reads=[bpcx], writes=[bcvo])
                m.op(dve, lambda: V.tensor_tensor(out=sq[:, cc, :], in0=cvo[:, cc, :], in1=cvo[:, cc, :], op=ALU.mult), reads=[bcvo], writes=[bsq])
            m.group(pe, [(lambda cc=cc: T.matmul(pS1[:], lhsT=L.onesf[:], rhs=cvo[:, cc, :], start=(cc == 0), stop=(cc == 3))) for cc in range(4)],
                    reads=[bcvo, Bc["onesf"]], writes=[bpS1])
            m.group(pe, [(lambda cc=cc: T.matmul(pS2[:], lhsT=L.onesf[:], rhs=sq[:, cc, :], start=(cc == 0), stop=(cc == 3))) for cc in range(4)],
                    reads=[bsq, Bc["onesf"]], writes=[bpS2])

        def ln_chain(g):
            gi = g % 2
            m.op(act, lambda: S.activation(out=mean[:], in_=pS1[:], func=AF.Identity, scale=1.0 / 512), reads=[bpS1], writes=[bmean])
            m.op(dve, lambda: V.tensor_tensor(out=tmp[:], in0=mean[:], in1=mean[:], op=ALU.mult), reads=[bmean], writes=[btmp])
            m.op(dve, lambda: V.scalar_tensor_tensor(out=rstd[:], in0=pS2[:], scalar=1.0 / 512, in1=tmp[:], op0=ALU.mult, op1=ALU.subtract),
                 reads=[bpS2, btmp], writes=[brstd])
            L.rstd_op(rstd[:], rstd[:], 128, [brstd], [brstd], 1.0, width=512)
            for cc in range(4):
                m.op(dve, lambda: V.tensor_tensor(out=tmp[:], in0=cvo[:, cc, :], in1=mean[:], op=ALU.subtract), reads=[bcvo, bmean], writes=[btmp])
                m.op(dve, lambda: V.tensor_tensor(out=tmp[:], in0=tmp[:], in1=rstd[:], op=ALU.mult), reads=[btmp, brstd], writes=[btmp])
                m.op(act, lambda: S.activation(out=cvT2[gi][:, cc, :], in_=tmp[:], func=AF.Silu, scale=L.colsA[:, C_LNG + cc:C_LNG + cc + 1],
                                               bias=L.colsA[:, C_LNB + cc:C_LNB + cc + 1]),
                     reads=[btmp, Bc["colsA"]], writes=[bcvT2[gi]])

        def merge_mix(g):
            gi = g % 2
            s = 0 if g * 512 < TS else 1
            for dc in range(8):
                pAx, bpAx, pBx, bpBx = (pA, bpA, pB, bpB) if dc % 2 == 0 else (pc, bpc, pc2, bpc2)
                m.group(pe, [(lambda kc=kc: T.matmul(pAx[:], lhsT=wap[:, kc, dc * 128:(dc + 1) * 128], rhs=onTw[gi][:, kc, :], start=(kc == 0), stop=(kc == 3))) for kc in range(4)],
                        reads=[bwap, bonTw[gi]], writes=[bpAx])
                m.group(pe, [(lambda kc=kc: T.matmul(pBx[:], lhsT=wcp[:, kc, dc * 128:(dc + 1) * 128], rhs=cvT2[gi][:, kc, :], start=(kc == 0), stop=(kc == 3))) for kc in range(4)],
                        reads=[bwcp, bcvT2[gi]], writes=[bpBx])
                m.op(dve, lambda: V.tensor_tensor(out=t1[:], in0=pAx[:], in1=sgw[gi][:, dc, :], op=ALU.mult), reads=[bpAx, bsgw[gi]], writes=[bt1])
                m.op(dve, lambda: V.tensor_tensor(out=t2[:], in0=pBx[:], in1=sgw[gi][:, 8 + dc, :], op=ALU.mult), reads=[bpBx, bsgw[gi]], writes=[bt2])
                m.op(pool, lambda: G.tensor_tensor(out=mergedT[:, dc, :], in0=t1[:], in1=t2[:], op=ALU.add), reads=[bt1, bt2], writes=[bmg])

        def mix(g):
            gi = g % 2
            s = 0 if g * 512 < TS else 1
            for i in range(4):
                tt = g * 4 + i
                xi = tt % 2
                if tt + 1 < NTOK // 128:
                    load_x(tt + 1)
                for half in range(2):
                    pm, bpm = pM[half], bpM[half]
                    m.group(pe, [(lambda kc=kc: T.matmul(pm[:], lhsT=mergedT[:, kc, i * 128:(i + 1) * 128], rhs=wout[:, kc, half * 512:(half + 1) * 512],
                                                         start=(kc == 0), stop=(kc == 7))) for kc in range(8)],
                            reads=[bmg, bwout], writes=[bpm])
                    m.op(dve, lambda: V.tensor_tensor(out=x1t[xi][:, half * 512:(half + 1) * 512], in0=pm[:], in1=G1b[:, s, half * 512:(half + 1) * 512], op=ALU.mult),
                         reads=[bpm, Bc["G1b"]], writes=[bx1t[xi]])
                m.op(pool, lambda: G.tensor_tensor(out=x1t[xi][:], in0=x1t[xi][:], in1=xt[xi][:], op=ALU.add), reads=[bx1t[xi], bxt[xi]], writes=[bx1t[xi]])
                m.dma(pool, L.x1_scr[tt * 128:(tt + 1) * 128, :], x1t[xi][:], reads=[bx1t[xi]], writes=[L.B_x1[tt]])

        load_group(0); load_rest(0)
        if L.NG > 1:
            load_group(1); load_rest(1)
        load_x(0)
        conv_ln(0)
        ln_chain(0)
        for g in range(L.NG):
            if g + 1 < L.NG:
                conv_ln(g + 1)
            if g + 2 < L.NG:
                load_group(g + 2)
            merge_mix(g)
            mix(g)
            if g + 1 < L.NG:
                ln_chain(g + 1)
            if g + 2 < L.NG:
                load_rest(g + 2)
    m.barrier()


def phase4(nc, m, ctx):
    L = _NS(ctx)
    pe, act, dve, pool, sp = m.pe, m.act, m.dve, m.pool, m.sp
    T, V, S, G = nc.tensor, nc.vector, nc.scalar, nc.gpsimd
    Bc = L.Bc
    with contextlib.ExitStack() as es:
        def sb(name, shape, dt):
            return es.enter_context(nc.sbuf_tensor(name, list(shape), dt))

        def ps(name, shape, dt):
            return es.enter_context(nc.psum_tensor(name, list(shape), dt))

        G2b = sb("G2b", [128, 2, D], F32)
        gate_bcast(nc, m, L, G2b, Bc["G2b"], 40)
        WFb = sb("WFb", [128, D], F32)
        m.dma(sp, WFb[:], L.w_final_norm.partition_broadcast(128), writes=[Bc["WFb"]])
        wup = sb("wup", [128, 8, 2 * DFF], BF16)
        bwup = [Buf() for _ in range(11)]
        wuv = L.w_up.rearrange("(k p) n -> p k n", p=128)
        for i in (0, 5, 1, 6, 2, 7, 3, 8, 4, 9, 10):
            m.dma(pool, wup[:, :, i * 512:(i + 1) * 512], wuv[:, :, i * 512:(i + 1) * 512], writes=[bwup[i]])
        wdn = sb("wdn", [128, NCH_FF, D], BF16)
        bwdn = [Buf(), Buf()]
        wdv = L.w_down.rearrange("(j p) n -> p j n", p=128)
        m.dma(pool, wdn[:, 0:11, :], wdv[:, 0:11, :], writes=[bwdn[0]])
        m.dma(pool, wdn[:, 11:22, :], wdv[:, 11:22, :], writes=[bwdn[1]])

        xh = [sb("xh0", [128, D], F32)] * 2
        bxh = [Buf()] * 2
        st = sb("st4", [128, 4], F32); bst = Buf()
        xn = sb("xn4", [128, D], BF16); bxn = Buf()
        h2T = sb("h2T", [128, 8, 514], BF16); bh2T = Buf()
        ta2 = [sb(f"ta{i}", [128, 512], F32) for i in range(2)]; bta2 = [Buf(), Buf()]
        tb2 = [sb(f"tb{i}", [128, 512], F32) for i in range(2)]; btb2 = [Buf(), Buf()]
        actT = sb("actT", [128, NCH_FF, 512], BF16); bactT = Buf()
        xr = [sb("xr0", [128, D], F32)] * 2
        bxr = [Buf()] * 2
        x2 = sb("x2", [128, D], F32); bx2 = Buf()
        yt = [sb("yt0", [128, D], F32)] * 2
        byt = [Buf()] * 2

        pT = ps("pT4", [128, 8, 128], BF16); bpT = PBuf()
        pa = [ps(f"pa4{i}", [128, 512], F32) for i in range(2)]
        bpa = [PBuf(), PBuf()]
        pb = [ps(f"pb4{i}", [128, 512], F32) for i in range(2)]
        bpb = [PBuf(), PBuf()]
        pD = [ps(f"pD{i}", [128, 512], F32) for i in range(2)]
        bpD = [PBuf(), PBuf()]

        windows = []
        a = 0
        while a < TS:
            W = min(510, TS - a)
            windows.append((a, W, 0, TS))
            a += W
        for j in range(NPR):
            p0 = TS + j * TP
            windows.append((p0, TP, p0, p0 + TP))

        st2 = sb("st4b", [128, 4], F32); bst2 = Buf()

        def win_tiles(w):
            a, W, lo, hi = w
            r0, r1 = max(a - 1, lo), min(a + W + 1, hi)
            out = []
            rr = r0
            while rr < r1:
                nr = min(128, r1 - rr)
                out.append((rr, nr))
                rr += nr
            return out

        def win_mtiles(w):
            a, W, lo, hi = w
            out = []
            mm = 0
            while mm < W:
                nm = min(128, W - mm)
                out.append((mm, nm))
                mm += nm
            return out

        def build_pads(w):
            a, W, lo, hi = w
            if a - 1 < lo:
                m.op(pool, lambda: G.memset(h2T[:, :, 0:1], 0.0), writes=[bh2T])
            if a + W >= hi:
                m.op(pool, lambda: G.memset(h2T[:, :, W + 1:W + 2], 0.0), writes=[bh2T])

        def build_pre(w, rr, nr):
            tiles = list(range(rr // 128, (rr + nr - 1) // 128 + 1))
            m.dma(sp, xh[0][0:nr, :], L.x1_scr[rr:rr + nr, :], reads=[L.B_x1[t] for t in tiles], writes=[bxh[0]])
            m.op(act, lambda: S.activation(out=xn[0:nr, :], in_=xh[0][0:nr, :], func=AF.Square, accum_out=st[0:nr, 0:1]),
                 reads=[bxh[0]], writes=[bxn, bst])
            L.rstd_op(st[0:nr, 1:2], st[0:nr, 0:1], nr, [bst], [bst], 1.0 / D)
            m.op(dve, lambda: V.tensor_scalar(out=xn[0:nr, :], in0=xh[0][0:nr, :], scalar1=st[0:nr, 1:2], scalar2=None, op0=ALU.mult),
                 reads=[bxh[0], bst], writes=[bxn])

        def build_post(w, rr, nr):
            a, W, lo, hi = w
            s = 0 if a < TS else 1
            m.group(pe, [(lambda k=k: T.transpose(out=pT[:, k, 0:nr], in_=xn[0:nr, k * 128:(k + 1) * 128], identity=L.identb[0:nr, 0:nr])) for k in range(8)],
                    reads=[bxn, Bc["identb"]], writes=[bpT])
            c0 = rr - (a - 1)
            for k in range(8):
                m.op(act, lambda k=k: S.activation(out=h2T[:, k, c0:c0 + nr], in_=pT[:, k, 0:nr], func=AF.Identity, scale=L.A2[:, s, k:k + 1],
                                                   bias=L.modT[:, 24 + k, s:s + 1]),
                     reads=[bpT, Bc["A2"], Bc["modT"]], writes=[bh2T])

        pend = [None]
        mid_hook = [None]

        def upproj(w):
            a, W, lo, hi = w
            N = W + 2
            for j in range(NCH_FF):
                pi = j % 2
                ca, cb = j * 128, DFF + j * 128
                m.group(pe, [(lambda k=k: T.matmul(pa[pi][:, 0:N], lhsT=wup[:, k, ca:ca + 128], rhs=h2T[:, k, 0:N], start=(k == 0), stop=(k == 7))) for k in range(8)],
                        reads=[bh2T, bwup[ca // 512]], writes=[bpa[pi]])
                m.group(pe, [(lambda k=k: T.matmul(pb[pi][:, 0:N], lhsT=wup[:, k, cb:cb + 128], rhs=h2T[:, k, 0:N], start=(k == 0), stop=(k == 7))) for k in range(8)],
                        reads=[bh2T, bwup[cb // 512]], writes=[bpb[pi]])
                if j == 12 and mid_hook[0] is not None:
                    mid_hook[0]()
                ta, bta, tb, btb = ta2[pi], bta2[pi], tb2[pi], btb2[pi]
                for (pp, bpp, tt_, btt, ch) in ((pa[pi], bpa[pi], ta, bta, j), (pb[pi], bpb[pi], tb, btb, NCH_FF + j)):
                    w1 = L.colsC[:, 1 * 44 + ch:1 * 44 + ch + 1]
                    m.op(act, lambda: S.activation(out=tt_[:, 0:W], in_=pp[:, 1:W + 1], func=AF.Identity, scale=w1), reads=[bpp, Bc["colsC"]], writes=[btt])
                for (pp, bpp, tt_, btt, ch) in ((pa[pi], bpa[pi], ta, bta, j), (pb[pi], bpb[pi], tb, btb, NCH_FF + j)):
                    w0 = L.colsC[:, 0 * 44 + ch:0 * 44 + ch + 1]
                    w2 = L.colsC[:, 2 * 44 + ch:2 * 44 + ch + 1]
                    m.op(dve, lambda: V.scalar_tensor_tensor(out=tt_[:, 0:W], in0=pp[:, 0:W], scalar=w0, in1=tt_[:, 0:W], op0=ALU.mult, op1=ALU.add),
                         reads=[bpp, Bc["colsC"], btt], writes=[btt])
                    m.op(dve, lambda: V.scalar_tensor_tensor(out=tt_[:, 0:W], in0=pp[:, 2:W + 2], scalar=w2, in1=tt_[:, 0:W], op0=ALU.mult, op1=ALU.add),
                         reads=[bpp, Bc["colsC"], btt], writes=[btt])

                def fin(jj=j, ta=ta, bta=bta, tb=tb, btb=btb):
                    m.op(act, lambda: S.activation(out=ta[:, 0:W], in_=ta[:, 0:W], func=AF.Silu), reads=[bta], writes=[bta])
                    m.op(pool, lambda: G.tensor_tensor(out=actT[:, jj, 0:W], in0=ta[:, 0:W], in1=tb[:, 0:W], op=ALU.mult), reads=[bta, btb], writes=[bactT])
                if pend[0] is not None:
                    pend[0]()
                pend[0] = fin
            pend[0]()
            pend[0] = None

        def down_tile(w, mm, nm):
            a, W, lo, hi = w
            s = 0 if a < TS else 1
            tok = a + mm
            tiles = list(range(tok // 128, (tok + nm - 1) // 128 + 1))
            m.dma(sp, xr[0][0:nm, :], L.x1_scr[tok:tok + nm, :], reads=[L.B_x1[t] for t in tiles], writes=[bxr[0]])
            for half in range(2):
                pd, bpd = pD[half], bpD[half]
                m.group(pe, [(lambda j=j: T.matmul(pd[0:nm, :], lhsT=actT[:, j, mm:mm + nm], rhs=wdn[:, j, half * 512:(half + 1) * 512],
                                                   start=(j == 0), stop=(j == NCH_FF - 1))) for j in range(NCH_FF)],
                        reads=[bactT, bwdn[0], bwdn[1]], writes=[bpd])
                m.op(dve, lambda: V.tensor_tensor(out=x2[0:nm, half * 512:(half + 1) * 512], in0=pd[0:nm, :], in1=G2b[0:nm, s, half * 512:(half + 1) * 512], op=ALU.mult),
                     reads=[bpd, Bc["G2b"]], writes=[bx2])
            m.op(pool, lambda: G.tensor_tensor(out=x2[0:nm, :], in0=x2[0:nm, :], in1=xr[0][0:nm, :], op=ALU.add), reads=[bx2, bxr[0]], writes=[bx2])
            m.op(act, lambda: S.activation(out=yt[0][0:nm, :], in_=x2[0:nm, :], func=AF.Square, accum_out=st2[0:nm, 0:1]), reads=[bx2], writes=[byt[0], bst2])
            L.rstd_op(st2[0:nm, 1:2], st2[0:nm, 0:1], nm, [bst2], [bst2], 1.0 / D)
            m.op(act, lambda: S.activation(out=yt[0][0:nm, :], in_=x2[0:nm, :], func=AF.Identity, scale=st2[0:nm, 1:2]), reads=[bx2, bst2], writes=[byt[0]])
            m.op(pool, lambda: G.tensor_tensor(out=yt[0][0:nm, :], in0=yt[0][0:nm, :], in1=WFb[0:nm, :], op=ALU.mult), reads=[byt[0], Bc["WFb"]], writes=[byt[0]])
            L.out_tks.append(m.dma(pool, L.y_out[tok:tok + nm, :], yt[0][0:nm, :], reads=[byt[0]]))

        build_pads(windows[0])
        for (rr, nr) in win_tiles(windows[0]):
            build_pre(windows[0], rr, nr)
            build_post(windows[0], rr, nr)
        for wi, w in enumerate(windows):
            nxt = windows[wi + 1] if wi + 1 < len(windows) else None
            nt = win_tiles(nxt) if nxt is not None else []
            mts = win_mtiles(w)
            mid_hook[0] = (lambda nxt=nxt, nt=nt: build_pre(nxt, *nt[0])) if nt else None
            upproj(w)
            if nxt is not None:
                build_pads(nxt)
            if nt:
                build_post(nxt, *nt[0])
            ti = 1
            for idx in range(len(mts)):
                if ti < len(nt):
                    build_pre(nxt, *nt[ti])
                down_tile(w, *mts[idx])
                if ti < len(nt):
                    build_post(nxt, *nt[ti])
                    ti += 1
            while ti < len(nt):
                build_pre(nxt, *nt[ti])
                build_post(nxt, *nt[ti])
                ti += 1
    m.barrier()


_NC = None


def _rope_tables():
    T_ = TS
    rows = T_ // 64
    row = np.repeat(np.arange(rows, dtype=np.float32), 64)
    col = np.tile(np.arange(64, dtype=np.float32), rows)
    half = 32
    freqs = (np.float32(10000.0) ** (-np.arange(0, half, 2, dtype=np.float32) / np.float32(half))).astype(np.float32)
    ang = np.stack([row[:, None] * freqs, col[:, None] * freqs], axis=1).astype(np.float32)
    cs = np.concatenate([np.cos(ang).reshape(T_, 32), np.sin(ang).reshape(T_, 32)], axis=1).astype(np.float32)
    return np.ascontiguousarray(cs)


def kernel(x_prompt, x_sample, cache_k, cache_v, c, c_ctx, w_ada, b_ada, w_norm1, w_in,
           lambda_q1, lambda_k1, lambda_q2, lambda_k2, w_head_norm, w_attn_proj,
           w_conv_dw, conv_ln_g, conv_ln_b, w_conv_proj, w_out, w_norm2, w_up,
           w_ffn_dw, w_down, w_final_norm):
    global _NC
    f = lambda a: np.ascontiguousarray(np.asarray(a, dtype=np.float32))
    x_prompt, x_sample, cache_k, cache_v, c, c_ctx = map(f, (x_prompt, x_sample, cache_k, cache_v, c, c_ctx))
    shared = {
        "rope_cs": _rope_tables(),
        "w_ada": f(w_ada)[0], "b_ada": f(b_ada)[0], "w_norm1": f(w_norm1)[0], "w_in": f(w_in)[0],
        "lambda_q1": f(lambda_q1)[0], "lambda_k1": f(lambda_k1)[0], "lambda_q2": f(lambda_q2)[0], "lambda_k2": f(lambda_k2)[0],
        "w_head_norm": f(w_head_norm)[0], "w_attn_proj": f(w_attn_proj)[0], "w_conv_dw": f(w_conv_dw)[0],
        "conv_ln_g": f(conv_ln_g)[0], "conv_ln_b": f(conv_ln_b)[0], "w_conv_proj": f(w_conv_proj)[0],
        "w_out": f(w_out)[0], "w_norm2": f(w_norm2)[0], "w_up": f(w_up)[0], "w_ffn_dw": f(w_ffn_dw)[0],
        "w_down": f(w_down)[0], "w_final_norm": f(w_final_norm),
    }
    in_maps = []
    for b in range(8):
        d = dict(shared)
        d["xs"] = np.ascontiguousarray(np.concatenate([x_sample[b], x_prompt[4 * b:4 * b + 4].reshape(NPR * TP, D)], axis=0))
        d["ck"] = np.ascontiguousarray(cache_k[b, 0].reshape(PAST, 512))
        d["cv"] = np.ascontiguousarray(cache_v[b, 0].reshape(PAST, 512))
        d["cvec"] = np.ascontiguousarray(np.concatenate([c[b], c_ctx], axis=0))
        in_maps.append(d)
    if _NC is None:
        _NC = build_nc()
    res = run_bass_kernel_spmd(_NC, in_maps, core_ids=list(range(8)))
    global _LAST
    _LAST = res
    y_prompt = np.zeros((32, TP, D), np.float32)
    y_sample = np.zeros((8, TS, D), np.float32)
    new_k = np.zeros((32, 1, TP, 4, 128), np.float32)
    new_v = np.zeros((32, 1, TP, 4, 128), np.float32)
    for b in range(8):
        r = res.results[b]
        y = np.asarray(r["y"])
        y_sample[b] = y[0:TS]
        y_prompt[4 * b:4 * b + 4] = y[TS:].reshape(NPR, TP, D)
        new_k[4 * b:4 * b + 4, 0] = np.asarray(r["nk"]).reshape(NPR, TP, 4, 128)
        new_v[4 * b:4 * b + 4, 0] = np.asarray(r["nv"]).reshape(NPR, TP, 4, 128)
    return (y_prompt, y_sample, new_k, new_v)
```

```python
import contextlib
import os
import numpy as np
import concourse.bass as bass
import concourse.mybir as mybir
from concourse.bass_utils import run_bass_kernel_spmd

F32 = mybir.dt.float32
BF16 = mybir.dt.bfloat16
AF = mybir.ActivationFunctionType
ALU = mybir.AluOpType

EPS = 1e-6
D = 1024
NTOK = 5120
TS = 4096
TP = 256
NPR = 4
PAST = 512
LAM_INIT = 0.8 - 0.6 * 1.0
DFF = 2816
NCH_FF = 22
DEBUG = bool(int(os.environ.get("MK_DEBUG", "0")))
UPTO = int(os.environ.get("MK_UPTO", "9"))
P1 = int(os.environ.get("MK_P1", "9"))


class Buf:
    __slots__ = ("name", "w", "r", "excl")

    def __init__(self, name="", excl=False):
        self.name = name
        self.w = {}
        self.r = {}
        self.excl = excl


def PBuf(name=""):
    return Buf(name, excl=True)


class Eng:
    def __init__(self, name, h, sem):
        self.name, self.h, self.sem = name, h, sem
        self.count = 0
        self.waited = {}

    def wait(self, tk):
        sem, val = tk
        k = sem.num
        if self.waited.get(k, 0) >= val:
            return
        self.h.wait_ge(sem, val)
        self.waited[k] = val


class MK:
    def __init__(self, nc, ndma=40):
        self.nc = nc
        self._cms = []
        self.pe = Eng("pe", nc.tensor, self._sem("s_pe"))
        self.act = Eng("act", nc.scalar, self._sem("s_act"))
        self.dve = Eng("dve", nc.vector, self._sem("s_dve"))
        self.pool = Eng("pool", nc.gpsimd, self._sem("s_pool"))
        self.sp = Eng("sp", nc.sync, self._sem("s_sp"))
        self.engs = [self.pe, self.act, self.dve, self.pool, self.sp]
        self.dsems = [self._sem(f"s_dma{i}") for i in range(ndma)]
        self.dcount = [0] * ndma
        self.nsw = 12
        self.dnext_hw = 0
        self.dnext_sw = 0

    def _sem(self, name):
        cm = self.nc.semaphore(name)
        s = cm.__enter__()
        self._cms.append(cm)
        return s

    def close(self):
        for cm in reversed(self._cms):
            cm.__exit__(None, None, None)

    def _deps(self, E, reads, writes):
        me = E.sem.num
        for b in reads:
            for k, tk in b.w.items():
                E.wait(tk)
            if b.excl:
                for k, tk in b.r.items():
                    if k != me:
                        E.wait(tk)
        skip_self = (E.name == "pe")
        for b in writes:
            for k, tk in b.w.items():
                if not (skip_self and k == me):
                    E.wait(tk)
            for k, tk in b.r.items():
                if not (skip_self and k == me):
                    E.wait(tk)

    def _mark(self, tk, reads, writes):
        k = tk[0].num
        for b in reads:
            b.r[k] = tk
        for b in writes:
            b.w[k] = tk
            b.r = {}

    def op(self, E, fn, reads=(), writes=()):
        self._deps(E, reads, writes)
        ins = fn()
        ins.then_inc(E.sem, 1)
        E.count += 1
        tk = (E.sem, E.count)
        self._mark(tk, reads, writes)
        return tk

    def group(self, E, fns, reads=(), writes=()):
        self._deps(E, reads, writes)
        ins = None
        for fn in fns:
            ins = fn()
        ins.then_inc(E.sem, 1)
        E.count += 1
        tk = (E.sem, E.count)
        self._mark(tk, reads, writes)
        return tk

    def dma(self, Q, out, in_, reads=(), writes=(), **kw):
        if Q.name == "pool":
            i = self.dnext_sw
            self.dnext_sw = (self.dnext_sw + 1) % self.nsw
        else:
            i = self.nsw + self.dnext_hw
            self.dnext_hw = (self.dnext_hw + 1) % (len(self.dsems) - self.nsw)
        sem = self.dsems[i]
        if self.dcount[i] > 0:
            Q.wait((sem, self.dcount[i]))
        for b in reads:
            for k, tk in b.w.items():
                Q.wait(tk)
        for b in writes:
            for k, tk in b.w.items():
                Q.wait(tk)
            for k, tk in b.r.items():
                Q.wait(tk)
        ins = Q.h.dma_start(out=out, in_=in_, **kw)
        ins.then_inc(sem, 16)
        self.dcount[i] += 16
        tk = (sem, self.dcount[i])
        self._mark(tk, reads, writes)
        return tk

    def barrier(self):
        tks = [(e.sem, e.count) for e in self.engs if e.count > 0]
        tks += [(s, c) for s, c in zip(self.dsems, self.dcount) if c > 0]
        for e in self.engs:
            for tk in tks:
                if tk[0].num != e.sem.num:
                    e.wait(tk)


C_BADA, C_WN1, C_WN2, C_LNG, C_LNB, C_CS, C_CC = 0, 48, 56, 64, 68, 72, 80
NROWS_A = 88


def build_nc():
    nc = bass.Bass("TRN2", target_bir_lowering=False)

    def din(name, shape):
        return nc.dram_tensor(name, list(shape), F32, kind="ExternalInput").ap()

    xs = din("xs", [NTOK, D])
    ck = din("ck", [PAST, 512])
    cv = din("cv", [PAST, 512])
    cvec = din("cvec", [2 * D])
    rope_cs = din("rope_cs", [TS, 64])
    w_ada = din("w_ada", [D, 6 * D])
    b_ada = din("b_ada", [6 * D])
    w_norm1 = din("w_norm1", [D])
    w_in = din("w_in", [D, 4608])
    lq1 = din("lambda_q1", [64]); lk1 = din("lambda_k1", [64])
    lq2 = din("lambda_q2", [64]); lk2 = din("lambda_k2", [64])
    w_head_norm = din("w_head_norm", [128])
    w_attn_proj = din("w_attn_proj", [512, D])
    w_conv_dw = din("w_conv_dw", [31, 512])
    conv_ln_g = din("conv_ln_g", [512]); conv_ln_b = din("conv_ln_b", [512])
    w_conv_proj = din("w_conv_proj", [512, D])
    w_out = din("w_out", [D, D])
    w_norm2 = din("w_norm2", [D])
    w_up = din("w_up", [D, 2 * DFF])
    w_ffn_dw = din("w_ffn_dw", [3, 2 * DFF])
    w_down = din("w_down", [DFF, D])
    w_final_norm = din("w_final_norm", [D])

    y_out = nc.dram_tensor("y", [NTOK, D], F32, kind="ExternalOutput").ap()
    nk_out = nc.dram_tensor("nk", [NPR * TP, 512], F32, kind="ExternalOutput").ap()
    nv_out = nc.dram_tensor("nv", [NPR * TP, 512], F32, kind="ExternalOutput").ap()

    skind = "ExternalOutput" if DEBUG else "Internal"
    qT_scr = nc.dram_tensor("qT_scr", [4, 128, NTOK], BF16, kind=skind).ap()
    kT_scr = nc.dram_tensor("kT_scr", [4, 128, NTOK], BF16, kind=skind).ap()
    v_scr = nc.dram_tensor("v_scr", [NTOK, 512], BF16, kind=skind).ap()
    gluT_scr = nc.dram_tensor("gluT_scr", [4, 128, NTOK], BF16, kind=skind).ap()
    sgT_scr = nc.dram_tensor("sgT_scr", [16, 128, NTOK], BF16, kind=skind).ap()
    onT_scr = nc.dram_tensor("onT_scr", [4, 128, NTOK], BF16, kind=skind).ap()
    x1_scr = nc.dram_tensor("x1_scr", [NTOK, D], F32, kind=skind).ap()

    m = MK(nc)
    pe, act, dve, pool, sp = m.pe, m.act, m.dve, m.pool, m.sp
    T, V, S, G = nc.tensor, nc.vector, nc.scalar, nc.gpsimd

    NG = NTOK // 512
    B_q = [Buf(f"q{g}") for g in range(NG)]
    B_k = [Buf(f"k{g}") for g in range(NG)]
    B_v = [Buf(f"v{t}") for t in range(NTOK // 128)]
    B_glu = [Buf(f"glu{g}") for g in range(NG)]
    B_sg = [Buf(f"sg{g}") for g in range(NG)]
    B_on = [[Buf(f"on{h}_{g}") for g in range(NG)] for h in range(4)]
    B_x1 = [Buf(f"x1_{t}") for t in range(NTOK // 128)]
    out_tks = []

    with contextlib.ExitStack() as es0:
        def sb0(name, shape, dt):
            return es0.enter_context(nc.sbuf_tensor(name, list(shape), dt))

        identf = sb0("identf", [128, 128], F32)
        identb = sb0("identb", [128, 128], BF16)
        onesf = sb0("onesf", [128, 128], F32)
        colsA = sb0("colsA", [128, NROWS_A], F32)
        colsB = sb0("colsB", [128, 124], F32)
        colsC = sb0("colsC", [128, 132], F32)
        modT = sb0("modT", [128, 48, 2], F32)
        A1 = sb0("A1", [128, 2, 8], F32); A2 = sb0("A2", [128, 2, 8], F32)
        WHb = sb0("WHb", [128, 128], F32)
        lamc = sb0("lamc", [128, 4], F32)
        nhalf = sb0("nhalf", [128, 8], F32)
        epsc = sb0("epsc", [128, 8], F32)
        Bc = {k: Buf(k) for k in ["identf", "identb", "onesf", "colsA", "colsB", "colsC", "modT", "A1", "A2",
                                  "G1b", "G2b", "WFb", "WHb", "lamc", "nhalf"]}

        def rstd_op(out, in_, n, reads, writes, scale, width=1):
            m.op(act, lambda: S.activation(out=out, in_=in_, func=AF.Ln, scale=scale, bias=epsc[0:n, 0:1]), reads=list(reads) + [Bc["nhalf"]], writes=writes)
            m.op(act, lambda: S.activation(out=out, in_=out, func=AF.Exp, scale=-0.5), reads=list(writes), writes=writes)

        with contextlib.ExitStack() as es:
            def sb(name, shape, dt):
                return es.enter_context(nc.sbuf_tensor(name, list(shape), dt))

            def ps(name, shape, dt):
                return es.enter_context(nc.psum_tensor(name, list(shape), dt))

            m.op(pool, lambda: G.memset(identf[:], 0.0), writes=[Bc["identf"]])
            m.op(pool, lambda: G.affine_select(out=identf[:], in_=identf[:], compare_op=ALU.not_equal, fill=1.0,
                                               base=0, pattern=[[-1, 128]], channel_multiplier=1),
                 reads=[Bc["identf"]], writes=[Bc["identf"]])
            m.op(dve, lambda: V.tensor_copy(out=identb[:], in_=identf[:]), reads=[Bc["identf"]], writes=[Bc["identb"]])
            m.op(pool, lambda: G.memset(onesf[:], 1.0), writes=[Bc["onesf"]])
            m.op(pool, lambda: G.memset(nhalf[:], -0.5), writes=[Bc["nhalf"]])
            m.op(pool, lambda: G.memset(epsc[:], EPS), writes=[Bc["nhalf"]])

            rowsA = sb("rowsA", [128, 128], F32)
            rowsB = sb("rowsB", [128, 128], F32)
            rowsC = sb("rowsC", [128, 128], F32)
            rowsC2 = sb("rowsC2", [8, 128], F32)
            bA, bB, bC, bC2 = Buf(), Buf(), Buf(), Buf()
            m.op(pool, lambda: G.memset(rowsA[:], 0.0), writes=[bA])

            def ldrows(dst, b, r0, vec, n):
                m.dma(sp, dst[r0:r0 + n, :], vec.rearrange("(r p) -> r p", p=128), writes=[b])
            ldrows(rowsA, bA, C_BADA, b_ada, 48)
            ldrows(rowsA, bA, C_WN1, w_norm1, 8)
            ldrows(rowsA, bA, C_WN2, w_norm2, 8)
            ldrows(rowsA, bA, C_LNG, conv_ln_g, 4)
            ldrows(rowsA, bA, C_LNB, conv_ln_b, 4)
            ldrows(rowsA, bA, C_CS, cvec, 16)
            m.dma(sp, rowsB[0:124, :], w_conv_dw.rearrange("j (c p) -> (j c) p", p=128), writes=[bB])
            wf = w_ffn_dw.rearrange("j (c p) -> (j c) p", p=128)
            m.dma(sp, rowsC[:, :], wf[0:128, :], writes=[bC])
            m.dma(sp, rowsC2[0:4, :], wf[128:132, :], writes=[bC2])
            m.dma(sp, WHb[:], w_head_norm.partition_broadcast(128), writes=[Bc["WHb"]])
            lam_in = sb("lam_in", [128, 4, 64], F32)
            bL = Buf()
            for i, lv in enumerate([lq1, lk1, lq2, lk2]):
                m.dma(sp, lam_in[:, i, :], lv.partition_broadcast(128), writes=[bL])

            pc0 = ps("pc0", [128, 512], F32)
            bp0 = PBuf()
            m.group(pe, [lambda: T.transpose(out=pc0[:, 0:NROWS_A], in_=rowsA[0:NROWS_A, :], identity=identf[0:NROWS_A, 0:NROWS_A])],
                    reads=[bA, Bc["identf"]], writes=[bp0])
            m.op(dve, lambda: V.tensor_copy(out=colsA[:], in_=pc0[:, 0:NROWS_A]), reads=[bp0], writes=[Bc["colsA"]])
            m.group(pe, [lambda: T.transpose(out=pc0[:, 0:124], in_=rowsB[0:124, :], identity=identf[0:124, 0:124])],
                    reads=[bB, Bc["identf"]], writes=[bp0])
            m.op(dve, lambda: V.tensor_copy(out=colsB[:], in_=pc0[:, 0:124]), reads=[bp0], writes=[Bc["colsB"]])
            m.group(pe, [lambda: T.transpose(out=pc0[:, 0:128], in_=rowsC[:, :], identity=identf[:, :]),
                         lambda: T.transpose(out=pc0[:, 128:132], in_=rowsC2[0:4, :], identity=identf[0:4, 0:4])],
                    reads=[bC, bC2, Bc["identf"]], writes=[bp0])
            m.op(dve, lambda: V.tensor_copy(out=colsC[:], in_=pc0[:, 0:132]), reads=[bp0], writes=[Bc["colsC"]])

            lt = sb("lt", [128, 2, 64], F32)
            ls = sb("ls", [128, 4], F32)
            bls = Buf()
            m.op(dve, lambda: V.tensor_tensor(out=lt[:, 0, :], in0=lam_in[:, 0, :], in1=lam_in[:, 1, :], op=ALU.mult), reads=[bL], writes=[bls])
            m.op(dve, lambda: V.tensor_tensor(out=lt[:, 1, :], in0=lam_in[:, 2, :], in1=lam_in[:, 3, :], op=ALU.mult), reads=[bL], writes=[bls])
            m.op(dve, lambda: V.reduce_sum(out=ls[:, 0:2], in_=lt[:], axis=mybir.AxisListType.X), reads=[bls], writes=[bls])
            m.op(act, lambda: S.activation(out=ls[:, 2:4], in_=ls[:, 0:2], func=AF.Exp), reads=[bls], writes=[bls])
            m.op(dve, lambda: V.tensor_tensor(out=lamc[:, 0:1], in0=ls[:, 2:3], in1=ls[:, 3:4], op=ALU.subtract), reads=[bls], writes=[Bc["lamc"]])
            m.op(dve, lambda: V.tensor_scalar(out=lamc[:, 0:1], in0=lamc[:, 0:1], scalar1=LAM_INIT, scalar2=None, op0=ALU.add), reads=[Bc["lamc"]], writes=[Bc["lamc"]])
            m.op(dve, lambda: V.tensor_scalar(out=lamc[:, 1:2], in0=lamc[:, 0:1], scalar1=-1.0, scalar2=None, op0=ALU.mult), reads=[Bc["lamc"]], writes=[Bc["lamc"]])
            m.op(dve, lambda: V.tensor_scalar(out=WHb[:], in0=WHb[:], scalar1=(1.0 - LAM_INIT), scalar2=None, op0=ALU.mult), reads=[Bc["WHb"]], writes=[Bc["WHb"]])

            scT = sb("scT", [128, 8, 2], F32)
            bsc = Buf()
            m.op(act, lambda: S.activation(out=scT[:].rearrange("p k s -> p s k"), in_=colsA[:, C_CS:C_CS + 16].rearrange("p (s k) -> p s k", s=2), func=AF.Silu),
                 reads=[Bc["colsA"]], writes=[bsc])
            pmod_full = ps("pmod", [128, 512], F32)
            pmod = pmod_full[:, 0:96].rearrange("p (j s) -> p j s", s=2)
            bpm = PBuf()
            wa = [sb(f"wa{i}", [128, 8, 1024], F32) for i in range(2)]
            bwa = [Buf(), Buf()]
            wav = w_ada.rearrange("(k p) n -> p k n", p=128)
            pm2 = ps("pm2", [128, 512], F32); bpm2 = PBuf()
            modrow = sb("modrow", [2, 6 * D], F32); bmr = Buf()
            for piece in range(6):
                wt_, bw = wa[piece % 2], bwa[piece % 2]
                m.dma(sp, wt_[:], wav[:, :, piece * 1024:(piece + 1) * 1024], writes=[bw])
                for nt in range(2):
                    m.group(pe, [(lambda k=k, nt=nt, wt_=wt_: T.matmul(pm2[0:2, :], lhsT=scT[:, k, :], rhs=wt_[:, k, nt * 512:(nt + 1) * 512],
                                                                         start=(k == 0), stop=(k == 7))) for k in range(8)],
                            reads=[bw, bsc], writes=[bpm2])
                    c0_ = piece * 1024 + nt * 512
                    m.op(dve, lambda c0_=c0_: V.tensor_copy(out=modrow[0:2, c0_:c0_ + 512], in_=pm2[0:2, :]), reads=[bpm2], writes=[bmr])
            m.group(pe, [(lambda j=j: T.transpose(out=pmod[:, j, :], in_=modrow[0:2, j * 128:(j + 1) * 128], identity=identf[0:2, 0:2])) for j in range(48)],
                    reads=[bmr, Bc["identf"]], writes=[bpm])
            for s in range(2):
                m.op(dve, lambda s=s: V.tensor_tensor(out=modT[:, :, s], in0=pmod[:, :, s], in1=colsA[:, C_BADA:C_BADA + 48], op=ALU.add),
                     reads=[bpm, Bc["colsA"]], writes=[Bc["modT"]])
            for s in range(2):
                m.op(dve, lambda s=s: V.scalar_tensor_tensor(out=A1[:, s, :], in0=modT[:, 8:16, s], scalar=1.0, in1=colsA[:, C_WN1:C_WN1 + 8], op0=ALU.add, op1=ALU.mult),
                     reads=[Bc["modT"], Bc["colsA"]], writes=[Bc["A1"]])
                m.op(dve, lambda s=s: V.scalar_tensor_tensor(out=A2[:, s, :], in0=modT[:, 32:40, s], scalar=1.0, in1=colsA[:, C_WN2:C_WN2 + 8], op0=ALU.add, op1=ALU.mult),
                     reads=[Bc["modT"], Bc["colsA"]], writes=[Bc["A2"]])
        m.barrier()

        if UPTO >= 1:
            phase1(nc, m, locals())
        with contextlib.ExitStack() as es23:
            p3pre = {}
            if UPTO >= 2:
                phase2(nc, m, locals())
            if UPTO >= 3:
                phase3(nc, m, locals())
        if UPTO >= 4:
            phase4(nc, m, locals())

        m.barrier()
    m.close()
    return nc


def gate_bcast(nc, m, L, gt, gbuf, c0):
    pe, dve = m.pe, m.dve
    T, V = nc.tensor, nc.vector
    Bc = L.Bc
    with contextlib.ExitStack() as es:
        dg = [es.enter_context(nc.sbuf_tensor(f"dg{c0}_{i}", [128, 128], F32)) for i in range(2)]
        pg = es.enter_context(nc.psum_tensor(f"pg{c0}", [128, D], F32))
        bdg = [Buf(), Buf()]
        bpg = PBuf()
        cnt = 0
        for s in range(2):
            for c in range(8):
                dgi, bd = dg[cnt % 2], bdg[cnt % 2]
                cnt += 1
                m.op(dve, lambda: V.tensor_scalar(out=dgi[:], in0=L.identf[:], scalar1=L.modT[:, c0 + c, s:s + 1], scalar2=None, op0=ALU.mult),
                     reads=[Bc["identf"], Bc["modT"]], writes=[bd])
                m.group(pe, [lambda: T.matmul(pg[:, c * 128:(c + 1) * 128], lhsT=L.onesf[:], rhs=dgi[:], start=True, stop=True)],
                        reads=[bd, Bc["onesf"]], writes=[bpg])
            m.op(dve, lambda: V.tensor_copy(out=gt[:, s, :], in_=pg[:]), reads=[bpg], writes=[gbuf])
    m.barrier()


class _NS:
    def __init__(self, d):
        self.__dict__.update(d)


def seg_type(tok):
    return 0 if tok < TS else 1


def phase1(nc, m, ctx):
    L = _NS(ctx)
    pe, act, dve, pool, sp = m.pe, m.act, m.dve, m.pool, m.sp
    T, V, S, G = nc.tensor, nc.vector, nc.scalar, nc.gpsimd
    Bc = L.Bc
    with contextlib.ExitStack() as es:
        def sb(name, shape, dt):
            return es.enter_context(nc.sbuf_tensor(name, list(shape), dt))

        def ps(name, shape, dt):
            return es.enter_context(nc.psum_tensor(name, list(shape), dt))

        win = sb("win", [128, 8, 4608], BF16)
        bwin = [Buf(f"win{i}") for i in range(9)]
        wv = L.w_in.rearrange("(k p) n -> p k n", p=128)
        for i in range(9):
            m.dma(pool, win[:, :, i * 512:(i + 1) * 512], wv[:, :, i * 512:(i + 1) * 512], writes=[bwin[i]])

        xt = [sb(f"xt{i}", [128, D], F32) for i in range(2)]
        bxt = [Buf(), Buf()]
        junk = sb("junk", [128, D], F32); bjunk = Buf()
        st = sb("st", [128, 4], F32); bst = Buf()
        xn = sb("xn", [128, D], BF16); bxn = Buf()
        hT = sb("hT", [128, 8, 512], BF16); bhT = Buf()
        cs_t = [sb(f"cs{i}", [128, 64], F32) for i in range(3)]
        bcs = [Buf(), Buf(), Buf()]
        r1 = sb("r1", [128, 256], F32); r2 = sb("r2", [128, 256], F32); br = Buf()
        qb = sb("qb", [128, 512], BF16); bqb = Buf()
        kb = sb("kb", [128, 512], BF16); bkb = Buf()
        vb = [sb(f"vb{i}", [128, 512], BF16) for i in range(2)]
        bvb = [Buf(), Buf()]
        kf = [sb(f"kf{i}", [128, 512], F32) for i in range(2)]
        bkf = [Buf(), Buf()]
        vf = [sb(f"vf{i}", [128, 512], F32) for i in range(2)]
        bvf = [Buf(), Buf()]
        qTg = [sb(f"qTg{i}", [128, 4, 512], BF16) for i in range(2)]
        bqTg = [Buf(), Buf()]
        kTg = [sb(f"kTg{i}", [128, 4, 512], BF16) for i in range(2)]
        bkTg = [Buf(), Buf()]
        sgm = sb("sgm", [128, 512], F32); bsgm = Buf()
        gluTg = [sb(f"gluTg{i}", [128, 4, 512], BF16) for i in range(2)]
        bglu = [Buf(), Buf()]
        sgTg = [sb(f"sgTg{i}", [128, 16, 512], BF16) for i in range(2)]
        bsg = [Buf(), Buf()]

        pT = ps("pT", [128, 8, 128], BF16); bpT = PBuf()
        pq = ps("pq", [128, 512], F32); bpq = PBuf()
        pk = ps("pk", [128, 512], F32); bpk = PBuf()
        pv = ps("pv", [128, 512], F32); bpv = PBuf()
        pqT = ps("pqT", [128, 2, 4, 128], BF16); bpqT = PBuf()
        pa = ps("pa", [128, 512], F32); bpa = PBuf()
        pb = ps("pb", [128, 512], F32); bpb = PBuf()
        pgg = ps("pgg", [128, 512], F32); bpgg = PBuf()

        def load_x(tt):
            i = tt % 2
            m.dma(sp, xt[i][:], L.xs[tt * 128:(tt + 1) * 128, :], writes=[bxt[i]])
            if tt * 128 < TS:
                m.dma(sp, cs_t[tt % 3][:], L.rope_cs[tt * 128:(tt + 1) * 128, :], writes=[bcs[tt % 3]])

        def rope(src_ps, bsrc, dst_bf, bdst, cst, bcst):
            sv = src_ps[:].rearrange("p (g a b i) -> p g a b i", g=8, a=2, b=2)
            dv = dst_bf[:].rearrange("p (g a b i) -> p g a b i", g=8, a=2, b=2)
            x1, x2 = sv[:, :, :, 0, :], sv[:, :, :, 1, :]
            o1, o2 = dv[:, :, :, 0, :], dv[:, :, :, 1, :]
            cos = cst[:, 0:32].rearrange("p (a i) -> p a i", a=2).unsqueeze(1).broadcast_to([128, 8, 2, 16])
            sin = cst[:, 32:64].rearrange("p (a i) -> p a i", a=2).unsqueeze(1).broadcast_to([128, 8, 2, 16])
            t1 = r1[:].rearrange("p (g a i) -> p g a i", g=8, a=2)
            t2 = r2[:].rearrange("p (g a i) -> p g a i", g=8, a=2)
            m.op(dve, lambda: V.tensor_tensor(out=t1, in0=x1, in1=cos, op=ALU.mult), reads=[bsrc, bcst], writes=[br])
            m.op(dve, lambda: V.tensor_tensor(out=t2, in0=x2, in1=sin, op=ALU.mult), reads=[bsrc, bcst], writes=[br])
            m.op(dve, lambda: V.tensor_tensor(out=o1, in0=t1, in1=t2, op=ALU.subtract), reads=[br], writes=[bdst])
            m.op(dve, lambda: V.tensor_tensor(out=t1, in0=x1, in1=sin, op=ALU.mult), reads=[bsrc, bcst], writes=[br])
            m.op(dve, lambda: V.tensor_tensor(out=t2, in0=x2, in1=cos, op=ALU.mult), reads=[bsrc, bcst], writes=[br])
            m.op(dve, lambda: V.tensor_tensor(out=o2, in0=t1, in1=t2, op=ALU.add), reads=[br], writes=[bdst])

        hT2 = [hT, sb("hTb", [128, 8, 512], BF16)]
        bhT2 = [bhT, Buf()]
        sg_banks = [(pgg, bpgg), (pa, bpa), (pb, bpb)]

        def fm_units(g):
            gi = g % 2
            hTg, bhTg = hT2[gi], bhT2[gi]
            t0 = g * 512
            units = []
            for cc in range(4):
                def u(cc=cc):
                    ca, cb = 1536 + cc * 128, 2048 + cc * 128
                    m.group(pe, [(lambda k=k: T.matmul(pa[:], lhsT=win[:, k, ca:ca + 128], rhs=hTg[:, k, :], start=(k == 0), stop=(k == 7))) for k in range(8)],
                            reads=[bhTg, bwin[3]], writes=[bpa])
                    m.group(pe, [(lambda k=k: T.matmul(pb[:], lhsT=win[:, k, cb:cb + 128], rhs=hTg[:, k, :], start=(k == 0), stop=(k == 7))) for k in range(8)],
                            reads=[bhTg, bwin[4]], writes=[bpb])
                    m.op(act, lambda: S.activation(out=sgm[:], in_=pb[:], func=AF.Sigmoid), reads=[bpb], writes=[bsgm])
                    m.op(dve, lambda: V.tensor_tensor(out=gluTg[gi][:, cc, :], in0=pa[:], in1=sgm[:], op=ALU.mult), reads=[bpa, bsgm], writes=[bglu[gi]])
                    if cc == 3:
                        m.dma(pool, L.gluT_scr[:, :, t0:t0 + 512].rearrange("c p t -> p c t"), gluTg[gi][:], reads=[bglu[gi]], writes=[L.B_glu[g]])
                units.append(u)
            for c16 in range(16):
                def u(c16=c16):
                    cg = 2560 + c16 * 128
                    pgx, bpgx = sg_banks[c16 % 3]
                    m.group(pe, [(lambda k=k: T.matmul(pgx[:], lhsT=win[:, k, cg:cg + 128], rhs=hTg[:, k, :], start=(k == 0), stop=(k == 7))) for k in range(8)],
                            reads=[bhTg, bwin[cg // 512]], writes=[bpgx])
                    m.op(act, lambda: S.activation(out=sgTg[gi][:, c16, :], in_=pgx[:], func=AF.Sigmoid), reads=[bpgx], writes=[bsg[gi]])
                    if c16 == 15:
                        m.dma(pool, L.sgT_scr[:, :, t0:t0 + 512].rearrange("c p t -> p c t"), sgTg[gi][:], reads=[bsg[gi]], writes=[L.B_sg[g]])
                units.append(u)
            return units

        NT = NTOK // 128

        def front(tt):
            g, i = tt // 4, tt % 4
            s = 0 if g * 512 < TS else 1
            hTg, bhTg = hT2[g % 2], bhT2[g % 2]
            xi = tt % 2
            if tt + 1 < NT:
                load_x(tt + 1)
            m.op(act, lambda: S.activation(out=junk[:], in_=xt[xi][:], func=AF.Square, accum_out=st[:, 0:1]),
                 reads=[bxt[xi]], writes=[bjunk, bst])
            L.rstd_op(st[:, 1:2], st[:, 0:1], 128, [bst], [bst], 1.0 / D)
            m.op(dve, lambda: V.tensor_scalar(out=xn[:], in0=xt[xi][:], scalar1=st[:, 1:2], scalar2=None, op0=ALU.mult),
                 reads=[bxt[xi], bst], writes=[bxn])
            m.group(pe, [(lambda k=k: T.transpose(out=pT[:, k, :], in_=xn[:, k * 128:(k + 1) * 128], identity=L.identb[:])) for k in range(8)],
                    reads=[bxn, Bc["identb"]], writes=[bpT])
            for k in range(8):
                m.op(act, lambda k=k: S.activation(out=hTg[:, k, i * 128:(i + 1) * 128], in_=pT[:, k, :], func=AF.Identity,
                                                   scale=L.A1[:, s, k:k + 1], bias=L.modT[:, k, s:s + 1]),
                     reads=[bpT, Bc["A1"], Bc["modT"]], writes=[bhTg])

        def tm_proj(tt):
            g, i = tt // 4, tt % 4
            hTg, bhTg = hT2[g % 2], bhT2[g % 2]
            for (pp, bpp, c0, wi) in ((pq, bpq, 0, 0), (pk, bpk, 512, 1), (pv, bpv, 1024, 2)):
                m.group(pe, [(lambda k=k, pp=pp, c0=c0: T.matmul(pp[:], lhsT=hTg[:, k, i * 128:(i + 1) * 128], rhs=win[:, k, c0:c0 + 512],
                                                                 start=(k == 0), stop=(k == 7))) for k in range(8)],
                        reads=[bhTg, bwin[wi]], writes=[bpp])

        def back(tt):
            g, i = tt // 4, tt % 4
            s = 0 if g * 512 < TS else 1
            gi = g % 2
            xi = tt % 2
            if s == 0:
                rope(pq, bpq, qb, bqb, cs_t[tt % 3], bcs[tt % 3])
                rope(pk, bpk, kb, bkb, cs_t[tt % 3], bcs[tt % 3])
            else:
                m.op(dve, lambda: V.tensor_copy(out=qb[:], in_=pq[:]), reads=[bpq], writes=[bqb])
                m.op(dve, lambda: V.tensor_copy(out=kb[:], in_=pk[:]), reads=[bpk], writes=[bkb])
                pj = tt % 2
                m.op(act, lambda: S.activation(out=kf[pj][:], in_=pk[:], func=AF.Identity), reads=[bpk], writes=[bkf[pj]])
                m.op(act, lambda: S.activation(out=vf[pj][:], in_=pv[:], func=AF.Identity), reads=[bpv], writes=[bvf[pj]])
                r0 = tt * 128 - TS
                L.out_tks.append(m.dma(pool, L.nk_out[r0:r0 + 128, :], kf[pj][:], reads=[bkf[pj]]))
                L.out_tks.append(m.dma(pool, L.nv_out[r0:r0 + 128, :], vf[pj][:], reads=[bvf[pj]]))
            vi = tt % 2
            m.op(act, lambda: S.activation(out=vb[vi][:], in_=pv[:], func=AF.Identity), reads=[bpv], writes=[bvb[vi]])
            m.dma(pool, L.v_scr[tt * 128:(tt + 1) * 128, :], vb[vi][:], reads=[bvb[vi]], writes=[L.B_v[tt]])
            m.group(pe, [(lambda h=h: T.transpose(out=pqT[:, 0, h, :], in_=qb[:, h * 128:(h + 1) * 128], identity=L.identb[:])) for h in range(4)]
                    + [(lambda h=h: T.transpose(out=pqT[:, 1, h, :], in_=kb[:, h * 128:(h + 1) * 128], identity=L.identb[:])) for h in range(4)],
                    reads=[bqb, bkb, Bc["identb"]], writes=[bpqT])
            m.op(act, lambda: S.activation(out=qTg[gi][:, :, i * 128:(i + 1) * 128], in_=pqT[:, 0, :, :], func=AF.Identity), reads=[bpqT], writes=[bqTg[gi]])
            m.op(act, lambda: S.activation(out=kTg[gi][:, :, i * 128:(i + 1) * 128], in_=pqT[:, 1, :, :], func=AF.Identity), reads=[bpqT], writes=[bkTg[gi]])
            if i == 3:
                t0 = g * 512
                m.dma(pool, L.qT_scr[:, :, t0:t0 + 512].rearrange("h p t -> p h t"), qTg[gi][:], reads=[bqTg[gi]], writes=[L.B_q[g]])
                m.dma(pool, L.kT_scr[:, :, t0:t0 + 512].rearrange("h p t -> p h t"), kTg[gi][:], reads=[bkTg[gi]], writes=[L.B_k[g]])

        load_x(0)
        pending = []
        front(0)
        for tt in range(NT):
            g, i = tt // 4, tt % 4
            tm_proj(tt)
            if tt + 1 < NT and not ((tt + 1) % 4 == 0 and pending):
                front(tt + 1)
                fronted = True
            else:
                fronted = False
            for _ in range(5):
                if pending:
                    pending.pop(0)()
            back(tt)
            if i == 3:
                while pending:
                    pending.pop(0)()
                pending = fm_units(g)
            if tt + 1 < NT and not fronted:
                front(tt + 1)
        while pending:
            pending.pop(0)()
    m.barrier()


def phase2(nc, m, ctx):
    L = _NS(ctx)
    pe, act, dve, pool, sp = m.pe, m.act, m.dve, m.pool, m.sp
    T, V, S, G = nc.tensor, nc.vector, nc.scalar, nc.gpsimd
    Bc = L.Bc
    VW = 132
    with contextlib.ExitStack() as es:
        def sb(name, shape, dt):
            return es.enter_context(nc.sbuf_tensor(name, list(shape), dt))

        def ps(name, shape, dt):
            return es.enter_context(nc.psum_tensor(name, list(shape), dt))

        if UPTO >= 3:
            def sb23(name, shape, dt):
                return L.es23.enter_context(nc.sbuf_tensor(name, list(shape), dt))
            G1b = sb23("G1b", [128, 2, D], F32)
            gate_bcast(nc, m, L, G1b, Bc["G1b"], 16)
            wap = sb23("wap", [128, 4, D], BF16); bwap = Buf()
            wcp = sb23("wcp", [128, 4, D], BF16); bwcp = Buf()
            wout = sb23("wout", [128, 8, D], BF16); bwout = Buf()
            m.dma(pool, wap[:], L.w_attn_proj.rearrange("(k p) n -> p k n", p=128), writes=[bwap])
            m.dma(pool, wcp[:], L.w_conv_proj.rearrange("(k p) n -> p k n", p=128), writes=[bwcp])
            m.dma(pool, wout[:], L.w_out.rearrange("(k p) n -> p k n", p=128), writes=[bwout])
            dgc = sb23("dgc", [128, 124, 128], BF16); bdgc = Buf()
            for jc in range(124):
                m.op(dve, lambda jc=jc: V.tensor_scalar(out=dgc[:, jc, :], in0=L.identf[:], scalar1=L.colsB[:, jc:jc + 1], scalar2=None, op0=ALU.mult),
                     reads=[Bc["identf"], Bc["colsB"]], writes=[bdgc])
            L.p3pre.update(dict(G1b=G1b, wap=wap, bwap=bwap, wcp=wcp, bwcp=bwcp, wout=wout, bwout=bwout, dgc=dgc, bdgc=bdgc))
        NKT = (TS + PAST) // 128
        kTh = [sb(f"kTh{i}", [128, TS + PAST], BF16) for i in range(2)]
        bkTh = [Buf(), Buf()]
        Vh = [sb(f"Vh{i}", [128, NKT, VW], BF16) for i in range(2)]
        bVh = [Buf(), Buf()]
        ckf = sb("ckf", [128, 4, 128], F32); bckf = Buf()
        qTt = [sb(f"qTt{i}", [128, 512], BF16) for i in range(2)]
        bqTt = [Buf(), Buf()]
        PT = [sb(f"PT{i}", [128, 1024], BF16) for i in range(3)]
        bPT = [Buf(), Buf(), Buf()]
        sm = sb("sm", [128, 16], F32); bsm = Buf()
        oa = sb("oa", [128, 4, 128], F32); boa = Buf()
        ob = sb("ob", [128, 4, 128], F32); bob = Buf()
        onb = sb("onb", [128, 4, 128], BF16); bonb = Buf()
        ocp = [sb(f"ocp{i}", [128, 8, 132], F32) for i in range(2)]
        bocp = [Buf(), Buf()]
        onTg = [sb(f"onTg{i}", [128, 512], BF16) for i in range(2)]
        bonTg = [Buf(), Buf()]

        pS = [ps(f"pS{i}", [128, 2, 512], F32) for i in range(2)]
        bpS = [PBuf(), PBuf()]
        pO = ps("pO", [128, 3, 512], F32); bpO = PBuf()
        pX = ps("pX", [128, 512], F32); bpX = PBuf()
        pXb = pX[:].bitcast(BF16)

        for i in range(2):
            m.op(pool, lambda i=i: G.memset(Vh[i][:, :, 128:VW], 1.0), writes=[bVh[i]])

        def acc_ap(a):
            return pO[:, a // 3, (a % 3) * VW:(a % 3) * VW + 129]

        pt_ctr = [0]

        qpre = {}
        s0_done = {}
        tail_pending = []
        mid_pending = []

        def s_group_x(hi_, qi_, nq_, kt):
            pi = kt % 2
            m.group(pe, [
                lambda: T.matmul(pS[pi][:, 0, 0:nq_], lhsT=kTh[hi_][0:64, kt * 128:(kt + 1) * 128], rhs=qTt[qi_][0:64, 0:nq_], start=True, stop=True),
                lambda: T.matmul(pS[pi][:, 1, 0:nq_], lhsT=kTh[hi_][64:128, kt * 128:(kt + 1) * 128], rhs=qTt[qi_][64:128, 0:nq_], start=True, stop=True),
            ], reads=[bkTh[hi_], bqTt[qi_]], writes=[bpS[pi]])

        def attend(hi, tok0, nq, nkt, on_buf_i, dest_tok0, h, Bon, nxt_q=None, nxt_s=None):
            nqc = nq // 128
            g = tok0 // 512
            qi = on_buf_i
            if not qpre.get((h, tok0)):
                m.dma(sp, qTt[qi][:, 0:nq], L.qT_scr[h, :, tok0:tok0 + nq], reads=[L.B_q[g]], writes=[bqTt[qi]])
            if nxt_q is not None:
                h2_, t2_, n2_ = nxt_q
                m.dma(sp, qTt[1 - qi][:, 0:n2_], L.qT_scr[h2_, :, t2_:t2_ + n2_], reads=[L.B_q[t2_ // 512]], writes=[bqTt[1 - qi]])
                qpre[(h2_, t2_)] = True

            def s_group(kt):
                s_group_x(hi, qi, nq, kt)

            if not s0_done.get((h, tok0)):
                s_group(0)
            for kt in range(nkt):
                if kt + 1 < nkt:
                    s_group(kt + 1)
                elif nxt_s is not None and nkt % 2 == 0:
                    hi2_, nq2_, h2_, t2_ = nxt_s
                    s_group_x(hi2_, 1 - qi, nq2_, 0)
                    s0_done[(h2_, t2_)] = True
                pi = kt % 2
                pti = pt_ctr[0] % 3
                pt_ctr[0] += 1
                m.op(act, lambda: S.activation(out=PT[pti][:].rearrange("p (a q) -> p a q", a=2)[:, :, 0:nq], in_=pS[pi][:, :, 0:nq], func=AF.Exp, scale=0.125),
                     reads=[bpS[pi]], writes=[bPT[pti]])
                fns = []
                seen = set()
                for smx in range(2):
                    for qc in range(nqc):
                        a = smx * 4 + qc
                        first = (kt == 0) and ((a // 3) not in seen)
                        seen.add(a // 3)
                        fns.append(lambda a=a, smx=smx, qc=qc, first=first: T.matmul(
                            acc_ap(a), lhsT=PT[pti][:, smx * 512 + qc * 128: smx * 512 + (qc + 1) * 128], rhs=Vh[hi][:, kt, 0:129],
                            start=first, stop=(kt == nkt - 1), skip_group_check=True))
                m.group(pe, fns, reads=[bPT[pti], bVh[hi]], writes=[bpO])
                if kt == min(10, nkt - 1) and mid_pending:
                    mid_pending.pop(0)()
                if kt == min(13, nkt - 1) and tail_pending and not mid_pending:
                    tail_pending.pop(0)()
            oi = on_buf_i
            oc = ocp[oi]
            boc = bocp[oi]
            if nqc == 4:
                for bk, (a0, na) in enumerate(((0, 3), (3, 3), (6, 2))):
                    m.op(dve, lambda: V.tensor_copy(out=oc[:, a0:a0 + na, 0:129], in_=pO[:, bk, 0:na * VW].rearrange("p (a w) -> p a w", w=VW)[:, :, 0:129]),
                         reads=[bpO], writes=[boc])
            else:
                m.op(dve, lambda: V.tensor_copy(out=oc[:, 0:2, 0:129], in_=pO[:, 0, 0:2 * VW].rearrange("p (a w) -> p a w", w=VW)[:, :, 0:129]), reads=[bpO], writes=[boc])
                m.op(dve, lambda: V.tensor_copy(out=oc[:, 4:6, 0:129], in_=pO[:, 1, VW:3 * VW].rearrange("p (a w) -> p a w", w=VW)[:, :, 0:129]), reads=[bpO], writes=[boc])
            O1 = oc[:, 0:nqc, 0:128]
            O2 = oc[:, 4:4 + nqc, 0:128]
            m.op(dve, lambda: V.reciprocal(out=sm[:, 0:nqc], in_=oc[:, 0:nqc, 128]), reads=[boc], writes=[bsm])
            m.op(dve, lambda: V.reciprocal(out=sm[:, 4:4 + nqc], in_=oc[:, 4:4 + nqc, 128]), reads=[boc], writes=[bsm])
            m.op(dve, lambda: V.tensor_scalar(out=sm[:, 4:4 + nqc], in0=sm[:, 4:4 + nqc], scalar1=L.lamc[:, 1:2], scalar2=None, op0=ALU.mult),
                 reads=[bsm, Bc["lamc"]], writes=[bsm])
            oav = oa[:, 0:nqc, :]
            obv = ob[:, 0:nqc, :]
            m.op(dve, lambda: V.tensor_tensor(out=oav, in0=O1, in1=sm[:, 0:nqc].unsqueeze(2).broadcast_to([128, nqc, 128]), op=ALU.mult),
                 reads=[boc, bsm], writes=[boa])
            m.op(dve, lambda: V.tensor_tensor(out=obv, in0=O2, in1=sm[:, 4:4 + nqc].unsqueeze(2).broadcast_to([128, nqc, 128]), op=ALU.mult),
                 reads=[boc, bsm], writes=[bob])
            m.op(dve, lambda: V.tensor_tensor(out=obv, in0=obv, in1=oav, op=ALU.add), reads=[bob, boa], writes=[bob])
            m.op(dve, lambda: V.tensor_tensor(out=oav, in0=obv, in1=obv, op=ALU.mult), reads=[bob], writes=[boa])
            m.op(dve, lambda: V.reduce_sum(out=sm[:, 8:8 + nqc], in_=oav, axis=mybir.AxisListType.X), reads=[boa], writes=[bsm])
            m.op(dve, lambda: V.tensor_scalar(out=sm[:, 12:12 + nqc], in0=sm[:, 8:8 + nqc], scalar1=1.0 / 128, scalar2=EPS, op0=ALU.mult, op1=ALU.add),
                 reads=[bsm], writes=[bsm])

            def mid():
                m.op(act, lambda: S.activation(out=sm[:, 12:12 + nqc], in_=sm[:, 12:12 + nqc], func=AF.Ln), reads=[bsm], writes=[bsm])
                m.op(act, lambda: S.activation(out=sm[:, 12:12 + nqc], in_=sm[:, 12:12 + nqc], func=AF.Exp, scale=-0.5), reads=[bsm], writes=[bsm])
                m.op(dve, lambda: V.tensor_tensor(out=obv, in0=obv, in1=sm[:, 12:12 + nqc].unsqueeze(2).broadcast_to([128, nqc, 128]), op=ALU.mult),
                     reads=[bob, bsm], writes=[bob])
                m.op(dve, lambda: V.tensor_tensor(out=onb[:, 0:nqc, :], in0=obv, in1=L.WHb[:].unsqueeze(1).broadcast_to([128, nqc, 128]), op=ALU.mult),
                     reads=[bob, Bc["WHb"]], writes=[bonb])
            mid_pending.append(mid)

            def tail():
                m.group(pe, [(lambda qc=qc: T.transpose(out=pXb[:, qc * 128:(qc + 1) * 128], in_=onb[:, qc, :], identity=L.identb[:])) for qc in range(nqc)],
                        reads=[bonb, Bc["identb"]], writes=[bpX])
                m.op(dve, lambda: V.tensor_copy(out=onTg[on_buf_i][:, 0:nq], in_=pXb[:, 0:nq]), reads=[bpX], writes=[bonTg[on_buf_i]])
                m.dma(pool, L.onT_scr[h, :, dest_tok0:dest_tok0 + nq], onTg[on_buf_i][:, 0:nq], reads=[bonTg[on_buf_i]], writes=[Bon])
            tail_pending.append(tail)

        cnt = 0

        def load_sample_head(h):
            hi = h % 2
            m.dma(sp, kTh[hi][:, 0:TS], L.kT_scr[h, :, 0:TS], reads=L.B_k[0:8], writes=[bkTh[hi]])
            for vq in range(4):
                m.dma(sp, Vh[hi][:, vq * 8:(vq + 1) * 8, 0:128],
                      L.v_scr[vq * 1024:(vq + 1) * 1024, h * 128:(h + 1) * 128].rearrange("(t p) d -> p t d", p=128),
                      reads=L.B_v[vq * 8:(vq + 1) * 8], writes=[bVh[hi]])
            m.dma(pool, Vh[hi][:, 32:36, 0:128], L.cv[:, h * 128:(h + 1) * 128].rearrange("(t p) d -> p t d", p=128), writes=[bVh[hi]])
            m.dma(sp, ckf[:], L.ck[:, h * 128:(h + 1) * 128].rearrange("(t p) d -> p t d", p=128), writes=[bckf])
            m.group(pe, [(lambda t=t: T.transpose(out=pX[:, t * 128:(t + 1) * 128], in_=ckf[:, t, :], identity=L.identf[:])) for t in range(4)],
                    reads=[bckf, Bc["identf"]], writes=[bpX])
            m.op(dve, lambda: V.tensor_copy(out=kTh[hi][:, TS:TS + PAST], in_=pX[:]), reads=[bpX], writes=[bkTh[hi]])

        def load_prompt_head(j, h):
            p0 = TS + j * TP
            hi = (j * 4 + h) % 2
            g = p0 // 512
            m.dma(sp, kTh[hi][:, 0:TP], L.kT_scr[h, :, p0:p0 + TP], reads=[L.B_k[g]], writes=[bkTh[hi]])
            m.dma(sp, Vh[hi][:, 0:2, 0:128], L.v_scr[p0:p0 + TP, h * 128:(h + 1) * 128].rearrange("(t p) d -> p t d", p=128),
                  reads=L.B_v[p0 // 128:p0 // 128 + 2], writes=[bVh[hi]])

        work = []
        for h in range(4):
            for qt in range(8):
                work.append(("s", 0, h, qt * 512, 512, NKT))
        for j in range(NPR):
            for h in range(4):
                work.append(("p", j, h, TS + j * TP, TP, 2))
        load_sample_head(0)
        loaded = {("s", 0, 0)}
        for wi_, (kind, j, h, tok0, nq, nkt) in enumerate(work):
            for (k2, j2, h2, *_r) in work[wi_ + 1:]:
                if (k2, j2, h2) != (kind, j, h):
                    if (k2, j2, h2) not in loaded:
                        if k2 == "s":
                            load_sample_head(h2)
                        else:
                            load_prompt_head(j2, h2)
                        loaded.add((k2, j2, h2))
                    break
            hi = (h % 2) if kind == "s" else ((j * 4 + h) % 2)
            nx = work[wi_ + 1] if wi_ + 1 < len(work) else None
            nxt_q = (nx[2], nx[3], nx[4]) if nx is not None else None
            g = tok0 // 512
            nxt_s = None
            if nx is not None:
                hi_n = (nx[2] % 2) if nx[0] == "s" else ((nx[1] * 4 + nx[2]) % 2)
                nxt_s = (hi_n, nx[4], nx[2], nx[3])
            attend(hi, tok0, nq, nkt, cnt % 2, tok0, h, L.B_on[h][g], nxt_q=nxt_q, nxt_s=nxt_s)
            cnt += 1
        while mid_pending:
            mid_pending.pop(0)()
        while tail_pending:
            tail_pending.pop(0)()
    m.barrier()


def phase3(nc, m, ctx):
    L = _NS(ctx)
    pe, act, dve, pool, sp = m.pe, m.act, m.dve, m.pool, m.sp
    T, V, S, G = nc.tensor, nc.vector, nc.scalar, nc.gpsimd
    Bc = L.Bc
    GW = 572
    with contextlib.ExitStack() as es:
        def sb(name, shape, dt):
            return es.enter_context(nc.sbuf_tensor(name, list(shape), dt))

        def ps(name, shape, dt):
            return es.enter_context(nc.psum_tensor(name, list(shape), dt))

        P3 = L.p3pre
        G1b, wap, bwap, wcp, bwcp, wout, bwout, dgc, bdgc = (P3[k] for k in ("G1b", "wap", "bwap", "wcp", "bwcp", "wout", "bwout", "dgc", "bdgc"))

        gluw = [sb(f"gluw{i}", [128, 4, GW], BF16) for i in range(2)]
        bgluw = [Buf(), Buf()]
        cvo = sb("cvo", [128, 4, 512], F32); bcvo = Buf()
        sq = sb("sq", [128, 4, 512], F32); bsq = Buf()
        mean = sb("mean", [128, 512], F32); bmean = Buf()
        rstd = sb("rstd", [128, 512], F32); brstd = Buf()
        tmp = sb("tmp3", [128, 512], F32); btmp = Buf()
        cvT = sb("cvT", [128, 4, 512], BF16); bcvT = Buf()
        onTw = [sb(f"onTw{i}", [128, 4, 512], BF16) for i in range(2)]
        bonTw = [Buf(), Buf()]
        sgw = [sb(f"sgw{i}", [128, 16, 512], BF16) for i in range(2)]
        bsgw = [Buf(), Buf()]
        t1 = sb("t1", [128, 512], F32); bt1 = Buf()
        t2 = sb("t2", [128, 512], F32); bt2 = Buf()
        mergedT = sb("mergedT", [128, 8, 512], BF16); bmg = Buf()
        xt = [sb(f"x3t{i}", [128, D], F32) for i in range(2)]
        bxt = [Buf(), Buf()]
        x1t = [sb(f"x1t{i}", [128, D], F32) for i in range(2)]
        bx1t = [Buf(), Buf()]

        pc = ps("pc", [128, 512], F32); bpc = PBuf()
        pc2 = ps("pc2", [128, 512], F32); bpc2 = PBuf()
        pS1 = ps("pSa", [128, 512], F32); bpS1 = PBuf()
        pS2 = ps("pSb", [128, 512], F32); bpS2 = PBuf()
        pA = ps("pA", [128, 512], F32); bpA = PBuf()
        pB = ps("pB", [128, 512], F32); bpB = PBuf()
        pM = [ps(f"pM{i}", [128, 512], F32) for i in range(2)]
        bpM = [PBuf(), PBuf()]

        def units_of(g):
            t0 = g * 512
            if t0 < TS:
                return [(t0, 512, 0, TS)]
            return [(t0, 256, t0, t0 + 256), (t0 + 256, 256, t0 + 256, t0 + 512)]

        def load_group(g):
            gi = g % 2
            t0 = g * 512
            units = units_of(g)
            full = all(u0 - 15 >= lo and u0 + n + 15 <= hi for (u0, n, lo, hi) in units)
            if not full:
                m.op(pool, lambda: G.memset(gluw[gi][:], 0.0), writes=[bgluw[gi]])
            for ui, (u0, n, lo, hi) in enumerate(units):
                a0, a1 = max(u0 - 15, lo), min(u0 + n + 15, hi)
                c0 = ui * 286 + (a0 - (u0 - 15))
                gs = list(range(a0 // 512, (a1 - 1) // 512 + 1))
                m.dma(sp, gluw[gi][:, :, c0:c0 + (a1 - a0)], L.gluT_scr[:, :, a0:a1].rearrange("c p t -> p c t"),
                      reads=[L.B_glu[x] for x in gs], writes=[bgluw[gi]])

        def load_rest(g):
            gi = g % 2
            t0 = g * 512
            m.dma(sp, onTw[gi][:], L.onT_scr[:, :, t0:t0 + 512].rearrange("c p t -> p c t"), reads=[L.B_on[h][g] for h in range(4)], writes=[bonTw[gi]])
            m.dma(sp, sgw[gi][:], L.sgT_scr[:, :, t0:t0 + 512].rearrange("c p t -> p c t"), reads=[L.B_sg[g]], writes=[bsgw[gi]])

        def load_x(tt):
            m.dma(sp, xt[tt % 2][:], L.xs[tt * 128:(tt + 1) * 128, :], writes=[bxt[tt % 2]])

        cvT2 = [cvT, sb("cvTb", [128, 4, 512], BF16)]
        bcvT2 = [bcvT, Buf()]

        def conv_ln(g):
            gi = g % 2
            s = 0 if g * 512 < TS else 1
            units = units_of(g)
            for cc in range(4):
                pcx, bpcx = (pc, bpc) if cc % 2 == 0 else (pc2, bpc2)
                for ui, (u0, n, lo, hi) in enumerate(units):
                    m.group(pe, [(lambda j=j: T.matmul(pcx[:, ui * 256:ui * 256 + n], lhsT=dgc[:, j * 4 + cc, :], rhs=gluw[gi][:, cc, ui * 286 + j:ui * 286 + j + n],
                                                       start=(j == 0), stop=(j == 30))) for j in range(31)],
                            reads=[bdgc, bgluw[gi]], writes=[bpcx])
                m.op(act, lambda: S.activation(out=cvo[:, cc, :], in_=pcx[:], func=AF.Identity), reads=[bpcx], writes=[bcvo])
                m.op(dve, lambda: V.tensor_tensor(out=sq[:, cc, :], in0=cvo[:, cc, :], in1=cvo[:, cc, :], op=ALU.mult), reads=[bcvo], writes=[bsq])
            m.group(pe, [(lambda cc=cc: T.matmul(pS1[:], lhsT=L.onesf[:], rhs=cvo[:, cc, :], start=(cc == 0), stop=(cc == 3))) for cc in range(4)],
                    reads=[bcvo, Bc["onesf"]], writes=[bpS1])
            m.group(pe, [(lambda cc=cc: T.matmul(pS2[:], lhsT=L.onesf[:], rhs=sq[:, cc, :], start=(cc == 0), stop=(cc == 3))) for cc in range(4)],
                    reads=[bsq, Bc["onesf"]], writes=[bpS2])

        def ln_chain(g):
            gi = g % 2
            m.op(act, lambda: S.activation(out=mean[:], in_=pS1[:], func=AF.Identity, scale=1.0 / 512), reads=[bpS1], writes=[bmean])
            m.op(dve, lambda: V.tensor_tensor(out=tmp[:], in0=mean[:], in1=mean[:], op=ALU.mult), reads=[bmean], writes=[btmp])
            m.op(dve, lambda: V.scalar_tensor_tensor(out=rstd[:], in0=pS2[:], scalar=1.0 / 512, in1=tmp[:], op0=ALU.mult, op1=ALU.subtract),
                 reads=[bpS2, btmp], writes=[brstd])
            L.rstd_op(rstd[:], rstd[:], 128, [brstd], [brstd], 1.0, width=512)
            for cc in range(4):
                m.op(dve, lambda: V.tensor_tensor(out=tmp[:], in0=cvo[:, cc, :], in1=mean[:], op=ALU.subtract), reads=[bcvo, bmean], writes=[btmp])
                m.op(dve, lambda: V.tensor_tensor(out=tmp[:], in0=tmp[:], in1=rstd[:], op=ALU.mult), reads=[btmp, brstd], writes=[btmp])
                m.op(act, lambda: S.activation(out=cvT2[gi][:, cc, :], in_=tmp[:], func=AF.Silu, scale=L.colsA[:, C_LNG + cc:C_LNG + cc + 1],
                                               bias=L.colsA[:, C_LNB + cc:C_LNB + cc + 1]),
                     reads=[btmp, Bc["colsA"]], writes=[bcvT2[gi]])

        def merge_mix(g):
            gi = g % 2
            s = 0 if g * 512 < TS else 1
            for dc in range(8):
                pAx, bpAx, pBx, bpBx = (pA, bpA, pB, bpB) if dc % 2 == 0 else (pc, bpc, pc2, bpc2)
                m.group(pe, [(lambda kc=kc: T.matmul(pAx[:], lhsT=wap[:, kc, dc * 128:(dc + 1) * 128], rhs=onTw[gi][:, kc, :], start=(kc == 0), stop=(kc == 3))) for kc in range(4)],
                        reads=[bwap, bonTw[gi]], writes=[bpAx])
                m.group(pe, [(lambda kc=kc: T.matmul(pBx[:], lhsT=wcp[:, kc, dc * 128:(dc + 1) * 128], rhs=cvT2[gi][:, kc, :], start=(kc == 0), stop=(kc == 3))) for kc in range(4)],
                        reads=[bwcp, bcvT2[gi]], writes=[bpBx])
                m.op(dve, lambda: V.tensor_tensor(out=t1[:], in0=pAx[:], in1=sgw[gi][:, dc, :], op=ALU.mult), reads=[bpAx, bsgw[gi]], writes=[bt1])
                m.op(dve, lambda: V.tensor_tensor(out=t2[:], in0=pBx[:], in1=sgw[gi][:, 8 + dc, :], op=ALU.mult), reads=[bpBx, bsgw[gi]], writes=[bt2])
                m.op(pool, lambda: G.tensor_tensor(out=mergedT[:, dc, :], in0=t1[:], in1=t2[:], op=ALU.add), reads=[bt1, bt2], writes=[bmg])

        def mix(g):
            gi = g % 2
            s = 0 if g * 512 < TS else 1
            for i in range(4):
                tt = g * 4 + i
                xi = tt % 2
                if tt + 1 < NTOK // 128:
                    load_x(tt + 1)
                for half in range(2):
                    pm, bpm = pM[half], bpM[half]
                    m.group(pe, [(lambda kc=kc: T.matmul(pm[:], lhsT=mergedT[:, kc, i * 128:(i + 1) * 128], rhs=wout[:, kc, half * 512:(half + 1) * 512],
                                                         start=(kc == 0), stop=(kc == 7))) for kc in range(8)],
                            reads=[bmg, bwout], writes=[bpm])
                    m.op(dve, lambda: V.tensor_tensor(out=x1t[xi][:, half * 512:(half + 1) * 512], in0=pm[:], in1=G1b[:, s, half * 512:(half + 1) * 512], op=ALU.mult),
                         reads=[bpm, Bc["G1b"]], writes=[bx1t[xi]])
                m.op(pool, lambda: G.tensor_tensor(out=x1t[xi][:], in0=x1t[xi][:], in1=xt[xi][:], op=ALU.add), reads=[bx1t[xi], bxt[xi]], writes=[bx1t[xi]])
                m.dma(pool, L.x1_scr[tt * 128:(tt + 1) * 128, :], x1t[xi][:], reads=[bx1t[xi]], writes=[L.B_x1[tt]])

        load_group(0); load_rest(0)
        if L.NG > 1:
            load_group(1); load_rest(1)
        load_x(0)
        conv_ln(0)
        ln_chain(0)
        for g in range(L.NG):
            if g + 1 < L.NG:
                conv_ln(g + 1)
            if g + 2 < L.NG:
                load_group(g + 2)
            merge_mix(g)
            mix(g)
            if g + 1 < L.NG:
                ln_chain(g + 1)
            if g + 2 < L.NG:
                load_rest(g + 2)
    m.barrier()


def phase4(nc, m, ctx):
    L = _NS(ctx)
    pe, act, dve, pool, sp = m.pe, m.act, m.dve, m.pool, m.sp
    T, V, S, G = nc.tensor, nc.vector, nc.scalar, nc.gpsimd
    Bc = L.Bc
    with contextlib.ExitStack() as es:
        def sb(name, shape, dt):
            return es.enter_context(nc.sbuf_tensor(name, list(shape), dt))

        def ps(name, shape, dt):
            return es.enter_context(nc.psum_tensor(name, list(shape), dt))

        G2b = sb("G2b", [128, 2, D], F32)
        gate_bcast(nc, m, L, G2b, Bc["G2b"], 40)
        WFb = sb("WFb", [128, D], F32)
        m.dma(sp, WFb[:], L.w_final_norm.partition_broadcast(128), writes=[Bc["WFb"]])
        wup = sb("wup", [128, 8, 2 * DFF], BF16)
        bwup = [Buf() for _ in range(11)]
        wuv = L.w_up.rearrange("(k p) n -> p k n", p=128)
        for i in (0, 5, 1, 6, 2, 7, 3, 8, 4, 9, 10):
            m.dma(pool, wup[:, :, i * 512:(i + 1) * 512], wuv[:, :, i * 512:(i + 1) * 512], writes=[bwup[i]])
        wdn = sb("wdn", [128, NCH_FF, D], BF16)
        bwdn = [Buf(), Buf()]
        wdv = L.w_down.rearrange("(j p) n -> p j n", p=128)
        m.dma(pool, wdn[:, 0:11, :], wdv[:, 0:11, :], writes=[bwdn[0]])
        m.dma(pool, wdn[:, 11:22, :], wdv[:, 11:22, :], writes=[bwdn[1]])

        xh = [sb("xh0", [128, D], F32)] * 2
        bxh = [Buf()] * 2
        st = sb("st4", [128, 4], F32); bst = Buf()
        xn = sb("xn4", [128, D], BF16); bxn = Buf()
        h2T = sb("h2T", [128, 8, 514], BF16); bh2T = Buf()
        ta2 = [sb(f"ta{i}", [128, 512], F32) for i in range(2)]; bta2 = [Buf(), Buf()]
        tb2 = [sb(f"tb{i}", [128, 512], F32) for i in range(2)]; btb2 = [Buf(), Buf()]
        actT = sb("actT", [128, NCH_FF, 512], BF16); bactT = Buf()
        xr = [sb("xr0", [128, D], F32)] * 2
        bxr = [Buf()] * 2
        x2 = sb("x2", [128, D], F32); bx2 = Buf()
        yt = [sb("yt0", [128, D], F32)] * 2
        byt = [Buf()] * 2

        pT = ps("pT4", [128, 8, 128], BF16); bpT = PBuf()
        pa = [ps(f"pa4{i}", [128, 512], F32) for i in range(2)]
        bpa = [PBuf(), PBuf()]
        pb = [ps(f"pb4{i}", [128, 512], F32) for i in range(2)]
        bpb = [PBuf(), PBuf()]
        pD = [ps(f"pD{i}", [128, 512], F32) for i in range(2)]
        bpD = [PBuf(), PBuf()]

        windows = []
        a = 0
        while a < TS:
            W = min(510, TS - a)
            windows.append((a, W, 0, TS))
            a += W
        for j in range(NPR):
            p0 = TS + j * TP
            windows.append((p0, TP, p0, p0 + TP))

        st2 = sb("st4b", [128, 4], F32); bst2 = Buf()

        def win_tiles(w):
            a, W, lo, hi = w
            r0, r1 = max(a - 1, lo), min(a + W + 1, hi)
            out = []
            rr = r0
            while rr < r1:
                nr = min(128, r1 - rr)
                out.append((rr, nr))
                rr += nr
            return out

        def win_mtiles(w):
            a, W, lo, hi = w
            out = []
            mm = 0
            while mm < W:
                nm = min(128, W - mm)
                out.append((mm, nm))
                mm += nm
            return out

        def build_pads(w):
            a, W, lo, hi = w
            if a - 1 < lo:
                m.op(pool, lambda: G.memset(h2T[:, :, 0:1], 0.0), writes=[bh2T])
            if a + W >= hi:
                m.op(pool, lambda: G.memset(h2T[:, :, W + 1:W + 2], 0.0), writes=[bh2T])

        def build_pre(w, rr, nr):
            tiles = list(range(rr // 128, (rr + nr - 1) // 128 + 1))
            m.dma(sp, xh[0][0:nr, :], L.x1_scr[rr:rr + nr, :], reads=[L.B_x1[t] for t in tiles], writes=[bxh[0]])
            m.op(act, lambda: S.activation(out=xn[0:nr, :], in_=xh[0][0:nr, :], func=AF.Square, accum_out=st[0:nr, 0:1]),
                 reads=[bxh[0]], writes=[bxn, bst])
            L.rstd_op(st[0:nr, 1:2], st[0:nr, 0:1], nr, [bst], [bst], 1.0 / D)
            m.op(dve, lambda: V.tensor_scalar(out=xn[0:nr, :], in0=xh[0][0:nr, :], scalar1=st[0:nr, 1:2], scalar2=None, op0=ALU.mult),
                 reads=[bxh[0], bst], writes=[bxn])

        def build_post(w, rr, nr):
            a, W, lo, hi = w
            s = 0 if a < TS else 1
            m.group(pe, [(lambda k=k: T.transpose(out=pT[:, k, 0:nr], in_=xn[0:nr, k * 128:(k + 1) * 128], identity=L.identb[0:nr, 0:nr])) for k in range(8)],
                    reads=[bxn, Bc["identb"]], writes=[bpT])
            c0 = rr - (a - 1)
            for k in range(8):
                m.op(act, lambda k=k: S.activation(out=h2T[:, k, c0:c0 + nr], in_=pT[:, k, 0:nr], func=AF.Identity, scale=L.A2[:, s, k:k + 1],
                                                   bias=L.modT[:, 24 + k, s:s + 1]),
                     reads=[bpT, Bc["A2"], Bc["modT"]], writes=[bh2T])

        pend = [None]
        mid_hook = [None]

        def upproj(w):
            a, W, lo, hi = w
            N = W + 2
            for j in range(NCH_FF):
                pi = j % 2
                ca, cb = j * 128, DFF + j * 128
                m.group(pe, [(lambda k=k: T.matmul(pa[pi][:, 0:N], lhsT=wup[:, k, ca:ca + 128], rhs=h2T[:, k, 0:N], start=(k == 0), stop=(k == 7))) for k in range(8)],
                        reads=[bh2T, bwup[ca // 512]], writes=[bpa[pi]])
                m.group(pe, [(lambda k=k: T.matmul(pb[pi][:, 0:N], lhsT=wup[:, k, cb:cb + 128], rhs=h2T[:, k, 0:N], start=(k == 0), stop=(k == 7))) for k in range(8)],
                        reads=[bh2T, bwup[cb // 512]], writes=[bpb[pi]])
                if j == 12 and mid_hook[0] is not None:
                    mid_hook[0]()
                ta, bta, tb, btb = ta2[pi], bta2[pi], tb2[pi], btb2[pi]
                for (pp, bpp, tt_, btt, ch) in ((pa[pi], bpa[pi], ta, bta, j), (pb[pi], bpb[pi], tb, btb, NCH_FF + j)):
                    w1 = L.colsC[:, 1 * 44 + ch:1 * 44 + ch + 1]
                    m.op(act, lambda: S.activation(out=tt_[:, 0:W], in_=pp[:, 1:W + 1], func=AF.Identity, scale=w1), reads=[bpp, Bc["colsC"]], writes=[btt])
                for (pp, bpp, tt_, btt, ch) in ((pa[pi], bpa[pi], ta, bta, j), (pb[pi], bpb[pi], tb, btb, NCH_FF + j)):
                    w0 = L.colsC[:, 0 * 44 + ch:0 * 44 + ch + 1]
                    w2 = L.colsC[:, 2 * 44 + ch:2 * 44 + ch + 1]
                    m.op(dve, lambda: V.scalar_tensor_tensor(out=tt_[:, 0:W], in0=pp[:, 0:W], scalar=w0, in1=tt_[:, 0:W], op0=ALU.mult, op1=ALU.add),
                         reads=[bpp, Bc["colsC"], btt], writes=[btt])
                    m.op(dve, lambda: V.scalar_tensor_tensor(out=tt_[:, 0:W], in0=pp[:, 2:W + 2], scalar=w2, in1=tt_[:, 0:W], op0=ALU.mult, op1=ALU.add),
                         reads=[bpp, Bc["colsC"], btt], writes=[btt])

                def fin(jj=j, ta=ta, bta=bta, tb=tb, btb=btb):
                    m.op(act, lambda: S.activation(out=ta[:, 0:W], in_=ta[:, 0:W], func=AF.Silu), reads=[bta], writes=[bta])
                    m.op(pool, lambda: G.tensor_tensor(out=actT[:, jj, 0:W], in0=ta[:, 0:W], in1=tb[:, 0:W], op=ALU.mult), reads=[bta, btb], writes=[bactT])
                if pend[0] is not None:
                    pend[0]()
                pend[0] = fin
            pend[0]()
            pend[0] = None

        def down_tile(w, mm, nm):
            a, W, lo, hi = w
            s = 0 if a < TS else 1
            tok = a + mm
            tiles = list(range(tok // 128, (tok + nm - 1) // 128 + 1))
            m.dma(sp, xr[0][0:nm, :], L.x1_scr[tok:tok + nm, :], reads=[L.B_x1[t] for t in tiles], writes=[bxr[0]])
            for half in range(2):
                pd, bpd = pD[half], bpD[half]
                m.group(pe, [(lambda j=j: T.matmul(pd[0:nm, :], lhsT=actT[:, j, mm:mm + nm], rhs=wdn[:, j, half * 512:(half + 1) * 512],
                                                   start=(j == 0), stop=(j == NCH_FF - 1))) for j in range(NCH_FF)],
                        reads=[bactT, bwdn[0], bwdn[1]], writes=[bpd])
                m.op(dve, lambda: V.tensor_tensor(out=x2[0:nm, half * 512:(half + 1) * 512], in0=pd[0:nm, :], in1=G2b[0:nm, s, half * 512:(half + 1) * 512], op=ALU.mult),
                     reads=[bpd, Bc["G2b"]], writes=[bx2])
            m.op(pool, lambda: G.tensor_tensor(out=x2[0:nm, :], in0=x2[0:nm, :], in1=xr[0][0:nm, :], op=ALU.add), reads=[bx2, bxr[0]], writes=[bx2])
            m.op(act, lambda: S.activation(out=yt[0][0:nm, :], in_=x2[0:nm, :], func=AF.Square, accum_out=st2[0:nm, 0:1]), reads=[bx2], writes=[byt[0], bst2])
            L.rstd_op(st2[0:nm, 1:2], st2[0:nm, 0:1], nm, [bst2], [bst2], 1.0 / D)
            m.op(act, lambda: S.activation(out=yt[0][0:nm, :], in_=x2[0:nm, :], func=AF.Identity, scale=st2[0:nm, 1:2]), reads=[bx2, bst2], writes=[byt[0]])
            m.op(pool, lambda: G.tensor_tensor(out=yt[0][0:nm, :], in0=yt[0][0:nm, :], in1=WFb[0:nm, :], op=ALU.mult), reads=[byt[0], Bc["WFb"]], writes=[byt[0]])
            L.out_tks.append(m.dma(pool, L.y_out[tok:tok + nm, :], yt[0][0:nm, :], reads=[byt[0]]))

        build_pads(windows[0])
        for (rr, nr) in win_tiles(windows[0]):
            build_pre(windows[0], rr, nr)
            build_post(windows[0], rr, nr)
        for wi, w in enumerate(windows):
            nxt = windows[wi + 1] if wi + 1 < len(windows) else None
            nt = win_tiles(nxt) if nxt is not None else []
            mts = win_mtiles(w)
            mid_hook[0] = (lambda nxt=nxt, nt=nt: build_pre(nxt, *nt[0])) if nt else None
            upproj(w)
            if nxt is not None:
                build_pads(nxt)
            if nt:
                build_post(nxt, *nt[0])
            ti = 1
            for idx in range(len(mts)):
                if ti < len(nt):
                    build_pre(nxt, *nt[ti])
                down_tile(w, *mts[idx])
                if ti < len(nt):
                    build_post(nxt, *nt[ti])
                    ti += 1
            while ti < len(nt):
                build_pre(nxt, *nt[ti])
                build_post(nxt, *nt[ti])
                ti += 1
    m.barrier()


_NC = None


def _rope_tables():
    T_ = TS
    rows = T_ // 64
    row = np.repeat(np.arange(rows, dtype=np.float32), 64)
    col = np.tile(np.arange(64, dtype=np.float32), rows)
    half = 32
    freqs = (np.float32(10000.0) ** (-np.arange(0, half, 2, dtype=np.float32) / np.float32(half))).astype(np.float32)
    ang = np.stack([row[:, None] * freqs, col[:, None] * freqs], axis=1).astype(np.float32)
    cs = np.concatenate([np.cos(ang).reshape(T_, 32), np.sin(ang).reshape(T_, 32)], axis=1).astype(np.float32)
    return np.ascontiguousarray(cs)


def kernel(x_prompt, x_sample, cache_k, cache_v, c, c_ctx, w_ada, b_ada, w_norm1, w_in,
           lambda_q1, lambda_k1, lambda_q2, lambda_k2, w_head_norm, w_attn_proj,
           w_conv_dw, conv_ln_g, conv_ln_b, w_conv_proj, w_out, w_norm2, w_up,
           w_ffn_dw, w_down, w_final_norm):
    global _NC
    f = lambda a: np.ascontiguousarray(np.asarray(a, dtype=np.float32))
    x_prompt, x_sample, cache_k, cache_v, c, c_ctx = map(f, (x_prompt, x_sample, cache_k, cache_v, c, c_ctx))
    shared = {
        "rope_cs": _rope_tables(),
        "w_ada": f(w_ada)[0], "b_ada": f(b_ada)[0], "w_norm1": f(w_norm1)[0], "w_in": f(w_in)[0],
        "lambda_q1": f(lambda_q1)[0], "lambda_k1": f(lambda_k1)[0], "lambda_q2": f(lambda_q2)[0], "lambda_k2": f(lambda_k2)[0],
        "w_head_norm": f(w_head_norm)[0], "w_attn_proj": f(w_attn_proj)[0], "w_conv_dw": f(w_conv_dw)[0],
        "conv_ln_g": f(conv_ln_g)[0], "conv_ln_b": f(conv_ln_b)[0], "w_conv_proj": f(w_conv_proj)[0],
        "w_out": f(w_out)[0], "w_norm2": f(w_norm2)[0], "w_up": f(w_up)[0], "w_ffn_dw": f(w_ffn_dw)[0],
        "w_down": f(w_down)[0], "w_final_norm": f(w_final_norm),
    }
    in_maps = []
    for b in range(8):
        d = dict(shared)
        d["xs"] = np.ascontiguousarray(np.concatenate([x_sample[b], x_prompt[4 * b:4 * b + 4].reshape(NPR * TP, D)], axis=0))
        d["ck"] = np.ascontiguousarray(cache_k[b, 0].reshape(PAST, 512))
        d["cv"] = np.ascontiguousarray(cache_v[b, 0].reshape(PAST, 512))
        d["cvec"] = np.ascontiguousarray(np.concatenate([c[b], c_ctx], axis=0))
        in_maps.append(d)
    if _NC is None:
        _NC = build_nc()
    res = run_bass_kernel_spmd(_NC, in_maps, core_ids=list(range(8)))
    global _LAST
    _LAST = res
    y_prompt = np.zeros((32, TP, D), np.float32)
    y_sample = np.zeros((8, TS, D), np.float32)
    new_k = np.zeros((32, 1, TP, 4, 128), np.float32)
    new_v = np.zeros((32, 1, TP, 4, 128), np.float32)
    for b in range(8):
        r = res.results[b]
        y = np.asarray(r["y"])
        y_sample[b] = y[0:TS]
        y_prompt[4 * b:4 * b + 4] = y[TS:].reshape(NPR, TP, D)
        new_k[4 * b:4 * b + 4, 0] = np.asarray(r["nk"]).reshape(NPR, TP, 4, 128)
        new_v[4 * b:4 * b + 4, 0] = np.asarray(r["nv"]).reshape(NPR, TP, 4, 128)
    return (y_prompt, y_sample, new_k, new_v)
```
